# Optimizing a Trainium2 kernel written in Bass

```python
import numpy as np
import jax
import jax.numpy as jnp
from jax import lax

D_MODEL = 4096
BATCH = 4
SEQ = 2048
DEPTH = 2

HEAD_DIM = 128
Q_BLK = 128

NSA_HEADS = 12
NSA_KV_HEADS = 3
NSA_GROUP = NSA_HEADS // NSA_KV_HEADS
CMP_LEN = 32
CMP_STRIDE = 16
SLC_BLK = 64
N_SEL = 16
WIN = 512
SLC_Q_CHUNK = 32

FOX_HEADS = 8

DIL_PAIRS = ((128, 1), (512, 4), (2048, 16))
DIL_GROUPS = 3
DIL_HEADS_PER_GROUP = 4
DIL_HEADS = DIL_GROUPS * DIL_HEADS_PER_GROUP

ALIBI_HEADS = NSA_HEADS + DIL_HEADS

NSA_Q_W = NSA_HEADS * HEAD_DIM
NSA_KV_W = NSA_KV_HEADS * HEAD_DIM
FOX_W = FOX_HEADS * HEAD_DIM
DIL_W = DIL_HEADS * HEAD_DIM
OFF_NSA_Q = 0
OFF_NSA_KV = OFF_NSA_Q + NSA_Q_W
OFF_NSA_GATE = OFF_NSA_KV + 6 * NSA_KV_W
OFF_FOX = OFF_NSA_GATE + 3 * NSA_HEADS
OFF_FOX_F = OFF_FOX + 3 * FOX_W
OFF_DIL = OFF_FOX_F + FOX_HEADS
IN_COLS = OFF_DIL + 3 * DIL_W

OUT_A = NSA_Q_W
OUT_B = FOX_W
OUT_C = DIL_HEADS_PER_GROUP * HEAD_DIM
BRANCH_ROWS = OUT_A + OUT_B + OUT_C

PEER_HEADS = 8
PEER_NKEYS = 128
PEER_EXPERTS = PEER_NKEYS ** 2
PEER_TOPK = 16
PEER_QDIM = 256
PEER_HALF = PEER_QDIM // 2
PEER_TOK_CHUNK = 64

ALPHA = (2.0 * DEPTH) ** 0.25
BETA = (8.0 * DEPTH) ** -0.25
LN_EPS = 1e-5
NEG = -1e30
BIG = 1e30

kernel_name = "hybrid_nsa_fox_dilated_peer_deepnorm_adaln"


def alibi_slopes(n):
    return (np.float32(2.0) ** (-8.0 * np.arange(1, n + 1, dtype=np.float32) / n)).astype(np.float32)


def layer_norm(h, g, b):
    hf = h.astype(jnp.float32)
    mu = hf.mean(-1, keepdims=True)
    var = jnp.square(hf - mu).mean(-1, keepdims=True)
    return ((hf - mu) * lax.rsqrt(var + LN_EPS) * g + b).astype(h.dtype)


def banded_attention(q, k, v, max_dist, slopes, step):
    B, H, T, hd = q.shape
    nprev = -(-max_dist // Q_BLK)
    nb = -(-T // Q_BLK)
    padq = nb * Q_BLK - T
    qb = jnp.pad(q, ((0, 0), (0, 0), (0, padq), (0, 0))).reshape(B, H, nb, Q_BLK, hd)

    def windows(a):
        ab = jnp.pad(a, ((0, 0), (0, 0), (nprev * Q_BLK, padq), (0, 0))).reshape(B, H, nb + nprev, Q_BLK, hd)
        return jnp.concatenate([ab[:, :, j:j + nb] for j in range(nprev + 1)], axis=3)

    kw, vw = windows(k), windows(v)
    qpos = np.arange(nb * Q_BLK).reshape(nb, Q_BLK)
    kpos = np.arange(nb)[:, None] * Q_BLK - nprev * Q_BLK + np.arange((nprev + 1) * Q_BLK)[None, :]
    dist = qpos[:, :, None] - kpos[:, None, :]
    valid = (dist >= 0) & (dist <= max_dist) & (kpos[:, None, :] >= 0)
    s = jnp.einsum('bhnqd,bhnkd->bhnqk', qb, kw).astype(jnp.float32) * (hd ** -0.5)
    s = s - (slopes * step)[None, :, None, None, None] * dist.astype(np.float32)
    s = jnp.where(valid, s, -jnp.inf)
    lse = jax.nn.logsumexp(s, axis=-1)
    p = jnp.exp(s - lse[..., None])
    out = jnp.einsum('bhnqk,bhnkd->bhnqd', p.astype(vw.dtype), vw)
    return (out.reshape(B, H, nb * Q_BLK, hd)[:, :, :T], lse.reshape(B, H, nb * Q_BLK)[:, :, :T])


def nsa_attention(q, k_cmp, v_cmp, k_slc, v_slc, k_win, v_win, gates, pe, w1, w2, slopes):
    B, H, T, hd = q.shape
    Hkv, G = NSA_KV_HEADS, NSA_GROUP
    scale = hd ** -0.5
    qg = q.reshape(B, Hkv, G, T, hd)
    pos = np.arange(T)

    n_cmp = (T - CMP_LEN) // CMP_STRIDE + 1
    blk_idx = np.arange(n_cmp)[:, None] * CMP_STRIDE + np.arange(CMP_LEN)[None, :]

    def compress(a, pe_i, w1_i, w2_i):
        blocks = a[:, :, blk_idx] + pe_i
        hmid = jax.nn.gelu(blocks.reshape(B, Hkv, n_cmp, CMP_LEN * hd) @ w1_i)
        return hmid @ w2_i

    kc = compress(k_cmp, pe[0], w1[0], w2[0])
    vc = compress(v_cmp, pe[1], w1[1], w2[1])
    cmp_end = np.arange(n_cmp) * CMP_STRIDE + CMP_LEN - 1
    dist_c = pos[:, None] - cmp_end[None, :]
    valid_c = dist_c >= 0
    s = jnp.einsum('bkgtd,bknd->bkgtn', qg, kc).astype(jnp.float32) * scale
    s = s - slopes[None, :, :, None, None] * dist_c.astype(np.float32)
    s = jnp.where(valid_c, s, NEG)
    e = jnp.exp(s - s.max(-1, keepdims=True)) * valid_c
    p_cmp = e / jnp.maximum(e.sum(-1, keepdims=True), 1e-30)
    o_cmp = jnp.einsum('bkgtn,bknd->bkgtd', p_cmp.astype(vc.dtype), vc)

    imp = p_cmp.sum(axis=2)
    n_slc = T // SLC_BLK
    r = SLC_BLK // CMP_STRIDE
    c_over = CMP_LEN // CMP_STRIDE
    imp_p = jnp.pad(imp, ((0, 0), (0, 0), (0, 0), (0, r * n_slc + r + c_over - n_cmp)))
    p_slc = jnp.zeros(imp.shape[:-1] + (n_slc,), jnp.float32)
    for m in range(r):
        for n in range(c_over):
            o = m + n
            p_slc = p_slc + imp_p[..., o:o + r * n_slc:r]
    blk = np.arange(n_slc)
    qblk = pos // SLC_BLK
    forced = (blk[None, :] == 0) | (blk[None, :] == qblk[:, None]) | (blk[None, :] == qblk[:, None] - 1)
    causal = blk[None, :] <= qblk[:, None]
    score = jnp.where(forced, BIG, jnp.where(causal, p_slc, NEG))
    n_top = min(N_SEL, n_slc)
    _, idx = lax.top_k(score, n_top)

    ks_b = k_slc.reshape(B, Hkv, n_slc, SLC_BLK, hd)
    vs_b = v_slc.reshape(B, Hkv, n_slc, SLC_BLK, hd)
    C = SLC_Q_CHUNK
    nc = T // C
    q_ch = qg.reshape(B, Hkv, G, nc, C, hd).transpose(3, 0, 1, 2, 4, 5)
    i_ch = idx.reshape(B, Hkv, nc, C, n_top).transpose(2, 0, 1, 3, 4)
    p_ch = jnp.arange(T, dtype=jnp.int32).reshape(nc, C)
    bi = jnp.arange(B)[:, None, None, None]
    hi = jnp.arange(Hkv)[None, :, None, None]
    offs = jnp.arange(SLC_BLK, dtype=jnp.int32)
    M = n_top * SLC_BLK

    def sel_chunk(args):
        qc, ic, pc = args
        kb = ks_b[bi, hi, ic].reshape(B, Hkv, C, M, hd)
        vb = vs_b[bi, hi, ic].reshape(B, Hkv, C, M, hd)
        kpos = (ic[..., None] * SLC_BLK + offs).reshape(B, Hkv, C, M)
        dist = (pc[None, None, :, None] - kpos)[:, :, None]
        ss = jnp.einsum('bkgcd,bkcmd->bkgcm', qc, kb).astype(jnp.float32) * scale
        ss = ss - slopes[None, :, :, None, None] * dist.astype(jnp.float32)
        ss = jnp.where(dist >= 0, ss, -jnp.inf)
        pp = jax.nn.softmax(ss, axis=-1)
        return jnp.einsum('bkgcm,bkcmd->bkgcd', pp.astype(vb.dtype), vb)

    o_slc = lax.map(sel_chunk, (q_ch, i_ch, p_ch))
    o_slc = o_slc.transpose(1, 2, 3, 0, 4, 5).reshape(B, H, T, hd)

    o_win, _ = banded_attention(q, jnp.repeat(k_win, G, axis=1), jnp.repeat(v_win, G, axis=1),
                                WIN - 1, slopes.reshape(H), 1)

    o_cmp = o_cmp.reshape(B, H, T, hd)
    return gates[..., 0:1] * o_cmp + gates[..., 1:2] * o_slc + gates[..., 2:3] * o_win


def fox_attention(q, k, v, log_f):
    B, H, T, hd = q.shape
    F = jnp.cumsum(log_f, axis=-1)
    nb = T // Q_BLK
    qb = q.reshape(B, H, nb, Q_BLK, hd).transpose(2, 0, 1, 3, 4)
    Fq = F.reshape(B, H, nb, Q_BLK).transpose(2, 0, 1, 3)
    qpos = jnp.arange(T, dtype=jnp.int32).reshape(nb, Q_BLK)
    kpos = jnp.arange(T, dtype=jnp.int32)

    def block(args):
        qi, Fi, pi = args
        s = jnp.einsum('bhqd,bhkd->bhqk', qi, k).astype(jnp.float32) * (hd ** -0.5)
        s = s + Fi[..., None] - F[:, :, None, :]
        s = jnp.where(pi[:, None] >= kpos[None, :], s, -jnp.inf)
        p = jax.nn.softmax(s, axis=-1)
        return jnp.einsum('bhqk,bhkd->bhqd', p.astype(v.dtype), v)

    out = lax.map(block, (qb, Fq, qpos))
    return out.transpose(1, 2, 0, 3, 4).reshape(B, H, T, hd)


def dilated_attention(q, k, v, slopes):
    _, B, Hg, T, hd = q.shape
    outs, lses = [], []
    for g, (window, dil) in enumerate(DIL_PAIRS):
        Ts = T // dil

        def sub(a):
            return a.reshape(B, Hg, Ts, dil, hd).transpose(0, 1, 3, 2, 4).reshape(B, Hg * dil, Ts, hd)

        o, lse = banded_attention(sub(q[g]), sub(k[g]), sub(v[g]), window // dil,
                                  jnp.repeat(slopes[g], dil), dil)
        outs.append(o.reshape(B, Hg, dil, Ts, hd).transpose(0, 1, 3, 2, 4).reshape(B, Hg, T, hd))
        lses.append(lse.reshape(B, Hg, dil, Ts).transpose(0, 1, 3, 2).reshape(B, Hg, T))
    w = jax.nn.softmax(jnp.stack(lses, axis=0), axis=0)
    return jnp.sum(w[..., None].astype(outs[0].dtype) * jnp.stack(outs, axis=0), axis=0)


def hybrid_mixer(u, w_in, b_forget, cmp_pe, cmp_w1, cmp_w2, w_branch, w_gate, b_gate, w_out,
                 nsa_slopes, dil_slopes):
    B, T, _ = u.shape
    hd = HEAD_DIM
    proj = u @ w_in

    q_a = proj[..., OFF_NSA_Q:OFF_NSA_Q + NSA_Q_W].reshape(B, T, NSA_HEADS, hd).transpose(0, 2, 1, 3)
    kv_a = proj[..., OFF_NSA_KV:OFF_NSA_KV + 6 * NSA_KV_W].reshape(B, T, 6, NSA_KV_HEADS, hd).transpose(2, 0, 3, 1, 4)
    g_a = jax.nn.sigmoid(proj[..., OFF_NSA_GATE:OFF_NSA_GATE + 3 * NSA_HEADS].reshape(B, T, NSA_HEADS, 3)).transpose(0, 2, 1, 3)
    o_a = nsa_attention(q_a, kv_a[0], kv_a[1], kv_a[2], kv_a[3], kv_a[4], kv_a[5], g_a,
                        cmp_pe, cmp_w1, cmp_w2, nsa_slopes)
    o_a = o_a.transpose(0, 2, 1, 3).reshape(B, T, OUT_A)

    qkv_b = proj[..., OFF_FOX:OFF_FOX + 3 * FOX_W].reshape(B, T, 3, FOX_HEADS, hd).transpose(2, 0, 3, 1, 4)
    log_f = jax.nn.log_sigmoid((proj[..., OFF_FOX_F:OFF_FOX_F + FOX_HEADS] + b_forget).astype(jnp.float32)).transpose(0, 2, 1)
    o_b = fox_attention(qkv_b[0], qkv_b[1], qkv_b[2], log_f).transpose(0, 2, 1, 3).reshape(B, T, OUT_B)

    qkv_c = proj[..., OFF_DIL:OFF_DIL + 3 * DIL_W].reshape(B, T, 3, DIL_GROUPS, DIL_HEADS_PER_GROUP, hd).transpose(2, 3, 0, 4, 1, 5)
    o_c = dilated_attention(qkv_c[0], qkv_c[1], qkv_c[2], dil_slopes).transpose(0, 2, 1, 3).reshape(B, T, OUT_C)

    y_a = o_a @ w_branch[:OUT_A]
    y_b = o_b @ w_branch[OUT_A:OUT_A + OUT_B]
    y_c = o_c @ w_branch[OUT_A + OUT_B:]
    ga, gb, gc = jnp.split(jax.nn.sigmoid(u @ w_gate + b_gate), 3, axis=-1)
    return (ga * y_a + gb * y_b + gc * y_c) @ w_out


def peer_ffn(u, wq, subkeys, table_u, table_v):
    B, T, D = u.shape
    N = B * T
    K = PEER_TOPK
    q = (u @ wq).reshape(N, PEER_HEADS, 2, PEER_HALF)
    s = jnp.einsum('nhpd,hpkd->nhpk', q, subkeys).astype(jnp.float32)
    s1, i1 = lax.top_k(s[:, :, 0], K)
    s2, i2 = lax.top_k(s[:, :, 1], K)
    cand = (s1[..., :, None] + s2[..., None, :]).reshape(N, PEER_HEADS, K * K)
    cidx = (i1[..., :, None] * PEER_NKEYS + i2[..., None, :]).reshape(N, PEER_HEADS, K * K)
    top_s, top_pos = lax.top_k(cand, K)
    expert = jnp.take_along_axis(cidx, top_pos, axis=-1)
    gate = jax.nn.softmax(top_s, axis=-1)
    C = PEER_TOK_CHUNK
    nc = N // C
    xs = u.reshape(nc, C, D)
    es = expert.reshape(nc, C, PEER_HEADS * K)
    gs = gate.reshape(nc, C, PEER_HEADS * K)

    def chunk(args):
        xc, ec, gc = args
        ue = table_u[ec]
        ve = table_v[ec]
        act = jax.nn.gelu(jnp.einsum('cd,ced->ce', xc, ue).astype(jnp.float32))
        return jnp.einsum('ce,ced->cd', (gc * act).astype(ve.dtype), ve)

    return lax.map(chunk, (xs, es, gs)).reshape(B, T, D)


def setup_inputs(seed: int = 0) -> dict:
    key = jax.random.key(seed)
    ks = jax.random.split(key, 24)
    L, D, hd = DEPTH, D_MODEL, HEAD_DIM

    def nrm(k, shape, std):
        return jax.random.normal(k, shape, jnp.float32) * std

    col = np.ones((IN_COLS,), np.float32)
    for i in (1, 3, 5):
        col[OFF_NSA_KV + i * NSA_KV_W:OFF_NSA_KV + (i + 1) * NSA_KV_W] = BETA
    col[OFF_FOX + 2 * FOX_W:OFF_FOX + 3 * FOX_W] = BETA
    col[OFF_DIL + 2 * DIL_W:OFF_DIL + 3 * DIL_W] = BETA
    row = np.concatenate([np.full((OUT_A,), OUT_A ** -0.5), np.full((OUT_B,), OUT_B ** -0.5),
                          np.full((OUT_C,), OUT_C ** -0.5)]).astype(np.float32) * BETA

    return {
        "x": nrm(ks[0], (BATCH, SEQ, D), 1.0),
        "c": nrm(ks[1], (BATCH, D), 1.0),
        "w_ada": nrm(ks[2], (L, D, 6 * D), 0.1 * D ** -0.5),
        "b_ada": nrm(ks[3], (L, 6 * D), 0.01),
        "w_in": nrm(ks[4], (L, D, IN_COLS), D ** -0.5) * jnp.asarray(col),
        "b_forget": jax.random.uniform(ks[5], (L, FOX_HEADS), jnp.float32, 1.0, 6.0),
        "cmp_pe": nrm(ks[6], (L, 2, CMP_LEN, hd), 0.02),
        "cmp_w1": nrm(ks[7], (L, 2, CMP_LEN * hd, hd), (CMP_LEN * hd) ** -0.5),
        "cmp_w2": nrm(ks[8], (L, 2, hd, hd), hd ** -0.5),
        "w_branch": nrm(ks[9], (L, BRANCH_ROWS, D), 1.0) * jnp.asarray(row)[:, None],
        "w_gate": nrm(ks[10], (L, D, 3 * D), D ** -0.5),
        "b_gate": nrm(ks[11], (L, 3 * D), 0.01),
        "w_out": nrm(ks[12], (L, D, D), BETA * D ** -0.5),
        "ln1_g": 1.0 + nrm(ks[13], (L, D), 0.02),
        "ln1_b": nrm(ks[14], (L, D), 0.02),
        "peer_wq": nrm(ks[15], (L, D, PEER_HEADS * PEER_QDIM), D ** -0.5),
        "peer_subkeys": nrm(ks[16], (L, PEER_HEADS, 2, PEER_NKEYS, PEER_HALF), PEER_HALF ** -0.5),
        "peer_u": nrm(ks[17], (L, PEER_EXPERTS, D), D ** -0.5),
        "peer_v": nrm(ks[18], (L, PEER_EXPERTS, D), BETA * (PEER_HEADS * PEER_TOPK) ** -0.5),
        "ln2_g": 1.0 + nrm(ks[19], (L, D), 0.02),
        "ln2_b": nrm(ks[20], (L, D), 0.02),
    }


def reference(x, c, w_ada, b_ada, w_in, b_forget, cmp_pe, cmp_w1, cmp_w2, w_branch, w_gate, b_gate,
              w_out, ln1_g, ln1_b, peer_wq, peer_subkeys, peer_u, peer_v, ln2_g, ln2_b):
    sl = alibi_slopes(ALIBI_HEADS)
    nsa_slopes = jnp.asarray(sl[0::2].reshape(NSA_KV_HEADS, NSA_GROUP))
    dil_slopes = jnp.asarray(sl[1::2].reshape(DIL_GROUPS, DIL_HEADS_PER_GROUP))
    for l in range(DEPTH):
        mod = c @ w_ada[l] + b_ada[l]
        sh1, sc1, g1, sh2, sc2, g2 = jnp.split(mod[:, None, :], 6, axis=-1)
        u = x * (1.0 + sc1) + sh1
        y = hybrid_mixer(u, w_in[l], b_forget[l], cmp_pe[l], cmp_w1[l], cmp_w2[l], w_branch[l],
                         w_gate[l], b_gate[l], w_out[l], nsa_slopes, dil_slopes)
        x = layer_norm(ALPHA * x + (1.0 + g1) * y, ln1_g[l], ln1_b[l])
        u = x * (1.0 + sc2) + sh2
        y = peer_ffn(u, peer_wq[l], peer_subkeys[l], peer_u[l], peer_v[l])
        x = layer_norm(ALPHA * x + (1.0 + g2) * y, ln2_g[l], ln2_b[l])
    return x
```

```python
import numpy as np
import concourse.bass as bass
import concourse.mybir as mybir

F32 = mybir.dt.float32
BF16 = mybir.dt.bfloat16
AF = mybir.ActivationFunctionType
ALU = mybir.AluOpType
AX = mybir.AxisListType


class Prog:
    def __init__(self, nc, n_dma_sp=24, n_dma_pool=12, n_dma_act=8):
        self.nc = nc
        self.ops = []
        self.lastw = {}
        self.readers = {}
        self.eng_last = {}
        self.dma_since_barrier = []
        self.ndma = {"sp": n_dma_sp, "pool": n_dma_pool, "act": n_dma_act}
        self.out_dmas = []

    def op(self, eng, fn, r=(), w=(), dma=False, extra=()):
        deps = set(extra)
        for k in r:
            if k in self.lastw:
                deps.add(self.lastw[k])
        for k in w:
            if k in self.lastw:
                deps.add(self.lastw[k])
            deps.update(self.readers.get(k, ()))
        idx = len(self.ops)
        self.ops.append(dict(eng=eng, fn=fn, deps=deps, dma=dma))
        for k in r:
            self.readers.setdefault(k, []).append(idx)
        for k in w:
            self.lastw[k] = idx
            self.readers[k] = []
        if dma:
            self.dma_since_barrier.append(idx)
        else:
            self.eng_last[eng] = idx
        return idx

    def dma(self, q, out, in_, r=(), w=(), is_out=False, **kw):
        def fn(e, out=out, in_=in_, kw=kw):
            return e.dma_start(out=out, in_=in_, **kw)
        i = self.op(q, fn, r=r, w=w, dma=True)
        if is_out:
            self.out_dmas.append(i)
        return i

    def barrier(self):
        deps = set(self.eng_last.values()) | set(self.dma_since_barrier)
        self.dma_since_barrier = []
        for eng in ("pe", "act", "dve", "pool", "sp"):
            idx = len(self.ops)
            self.ops.append(dict(eng=eng, fn=None, deps=set(deps), dma=False, barrier=True))
            self.eng_last[eng] = idx
        self.lastw = {}
        self.readers = {}

    def emit(self, stack):
        nc = self.nc
        engs = {"pe": nc.tensor, "act": nc.scalar, "dve": nc.vector, "pool": nc.gpsimd, "sp": nc.sync}
        csem = {e: stack.enter_context(nc.semaphore("c_" + e)) for e in ("pe", "act", "dve", "pool")}
        dsems = {q: [stack.enter_context(nc.semaphore(f"d_{q}{i}")) for i in range(n)]
                 for q, n in self.ndma.items()}
        ops = self.ops
        hasdep = [False] * len(ops)
        for o in ops:
            for d in o["deps"]:
                if ops[d]["eng"] == "pe" and o["eng"] == "pe" and not o["dma"] and not ops[d]["dma"] \
                        and not o.get("barrier"):
                    continue
                hasdep[d] = True
        for i in self.out_dmas:
            hasdep[i] = True
        ccount = {e: 0 for e in csem}
        dnext = {q: 0 for q in dsems}
        duse = {q: [0] * len(dsems[q]) for q in dsems}
        comp = [None] * len(ops)
        pre_wait = [None] * len(ops)
        for i, o in enumerate(ops):
            e = o["eng"]
            if o.get("barrier"):
                continue
            if o["dma"]:
                s = dnext[e]
                dnext[e] = (s + 1) % len(dsems[e])
                if duse[e][s] > 0:
                    pre_wait[i] = (dsems[e][s], 16 * duse[e][s])
                duse[e][s] += 1
                comp[i] = (dsems[e][s], 16 * duse[e][s], None)
            else:
                if hasdep[i]:
                    ccount[e] += 1
                    comp[i] = (csem[e], ccount[e], e)
        per_eng = {e: [] for e in engs}
        for i, o in enumerate(ops):
            per_eng[o["eng"]].append(i)
        block = stack.enter_context(nc.Block())
        final_waits = [comp[i][:2] for i in self.out_dmas]
        self.n_instr = 0

        def make(ename):
            def body(eng):
                known = {}

                def wait(sem, val):
                    k = id(sem)
                    if known.get(k, 0) >= val:
                        return
                    known[k] = val
                    eng.wait_ge(sem, val)
                    self.n_instr += 1
                for i in per_eng[ename]:
                    o = ops[i]
                    for d in sorted(o["deps"]):
                        c = comp[d]
                        if c is None:
                            continue
                        if c[2] == "pe" and ename == "pe" and not o["dma"]:
                            continue
                        wait(c[0], c[1])
                    if o.get("barrier"):
                        continue
                    if pre_wait[i] is not None:
                        wait(*pre_wait[i])
                    ins = o["fn"](eng)
                    self.n_instr += 1
                    if o["dma"]:
                        ins.then_inc(comp[i][0], 16)
                    elif comp[i] is not None:
                        ins.then_inc(comp[i][0], 1)
                if ename == "sp":
                    for s, v in final_waits:
                        wait(s, v)
            return body
        block.tensor(make("pe"))
        block.scalar(make("act"))
        block.vector(make("dve"))
        block.gpsimd(make("pool"))
        block.sync(make("sp"))


class Arena:
    def __init__(self, ap, ncols):
        self.ap = ap
        self.ncols = ncols
        self.off = 0

    def reset(self):
        self.off = 0

    def alloc(self, cols, parts=128):
        cols_al = (cols + 7) // 8 * 8
        assert self.off + cols_al <= self.ncols, (self.off, cols, self.ncols)
        v = self.ap[0:parts, self.off:self.off + cols]
        self.off += cols_al
        return v


from contextlib import ExitStack

D = 4096
T = 2048
NT = 4
HD = 128
IN_COLS = 11564
L = 2
KC = D // 128

SEG = dict(qa=(0, 1536), kcmp=(1536, 1920), vcmp=(1920, 2304), kslc=(2304, 2688), vslc=(2688, 3072),
           kwin=(3072, 3456), vwin=(3456, 3840), gates=(3840, 3876), foxq=(3876, 4900), foxk=(4900, 5924),
           foxv=(5924, 6948), forget=(6948, 6956), dilq=(6956, 8492), dilk=(8492, 10028), dilv=(10028, 11564))
F_SEGS = ["qa", "kcmp", "vcmp", "kslc", "kwin", "foxq", "foxk", "dilq", "dilk", "forget"]
T_SEGS = ["vslc", "vwin", "gates", "foxv", "dilv"]
Q_SEGS = ["qa", "gates", "foxq", "dilq"]


class SB:
    def __init__(self, arena):
        self.a = arena
        self.off = 0
        self.n = arena.shape[1]

    def reset(self):
        self.off = 0

    def f32(self, cols, parts=128):
        c = (cols + 1) // 2 * 2
        assert self.off + c <= self.n, ("SBUF arena overflow", self.off, c)
        v = self.a[0:parts, self.off:self.off + cols]
        self.off += c
        return v

    def bf16(self, cols, parts=128):
        c = (cols + 3) // 4 * 2
        assert self.off + c <= self.n, ("SBUF arena overflow", self.off, c)
        v = self.a[0:parts, self.off:self.off + c].bitcast(BF16)[:, 0:cols]
        self.off += c
        return v


EXTRA_INPUTS = {}


class K:
    pass


def build(stages=None, debug=(), inject=(), layers=(0,), own_by_layer=None, final_out=False):
    nc = bass.Bass("TRN2", target_bir_lowering=False)
    k = K()
    k.nc = nc
    k.debug = set(debug)
    k.inject = set(inject)
    k.dbg_out = {}

    def din(name, shape, dt=F32):
        return nc.dram_tensor(name, list(shape), dt, kind="ExternalInput").ap()

    def scr(name, shape, dt=F32):
        kind = "ExternalOutput" if name in k.debug else ("ExternalInput" if name in k.inject else "Internal")
        return nc.dram_tensor(name, list(shape), dt, kind=kind).ap()
    shapes = dict(xctx=([T, D], F32), cT=([128, KC], F32), ident=([128, 128], F32), tabs=([128, TABW], F32),
                  kbias=([128, 16], F32), dcmp=([128, T], F32), cm=([128, 32], F32), Emat=([32, T], BF16),
                  smul=([T, 32], F32), sadd=([T, 32], F32), tri=([128, 5, 512], F32),
                  w_ada=([L, D, 6 * D], F32), b_ada=([L, 6 * D], F32), w_in=([L, D, IN_COLS], F32),
                  b_forget=([L, 8], F32), cmp_peT=([L, 2, 128, 32], F32), cmp_w1=([L, 2, 4096, 128], F32),
                  cmp_w2=([L, 2, 128, 128], F32), b_gateT=([L, 128, 96], F32), w_gate=([L, D, 3 * D], F32),
                  w_branch=([L, 3072, D], F32), w_out=([L, D, D], F32), ln1_g=([L, D], F32), ln1_b=([L, D], F32),
                  ln2_g=([L, D], F32), ln2_b=([L, D], F32), peer_wq=([L, D, 2048], F32),
                  peer_skT=([L, 16, 128, 128], F32), peer_u=([L, 16384, D], F32), peer_v=([L, 16384, D], F32),
                  oh16=([16, 8, 128], BF16))
    shapes.update(EXTRA_INPUTS)

    class Lazy(dict):
        def __missing__(self, name):
            sh, dt = shapes[name]
            self[name] = din(name, sh, dt)
            return self[name]
    I = Lazy()
    k.I = I
    S = {}
    S["mod"] = scr("mod", [L, 6 * D])
    S["uT"] = scr("uT", [D, T], BF16)
    for s in F_SEGS:
        S[s] = scr("p_" + s, [SEG[s][1] - SEG[s][0], T], F32 if s == "forget" else BF16)
    for s in T_SEGS:
        S[s] = scr("p_" + s, [T, SEG[s][1] - SEG[s][0]], F32 if s in ("gates", "forget") else BF16)
    S["kc"] = scr("kc", [3, 128, 128], BF16)
    S["vc"] = scr("vc", [3, 128, 128], F32)
    S["Frow"] = scr("Frow", [8, T])
    S["nFrow"] = scr("nFrow", [8, T])
    S["oT"] = scr("oT", [3072, T], BF16)
    S["gT"] = scr("gT", [3 * D, T], BF16)
    for nm in ("ya", "yb", "yc", "mT"):
        S[nm] = scr(nm, [D, T], BF16)
    for nm in ("y", "x1", "y2", "x2"):
        S[nm] = scr(nm, [T, D], F32)
    S["qT"] = scr("qT", [2048, T], F32)
    S["S2T"] = scr("S2T", [8, 128, T], F32)
    S["S1H"] = scr("S1H", [4, 8, 128, T], BF16)
    S["GA"] = scr("GA", [128, 128, T], BF16)
    k.out = nc.dram_tensor("out", [1024, D], F32, kind="ExternalOutput").ap()
    k.S = S
    if stages is None:
        stages = ["mod", "modulate", "win", "cmp", "cumsum", "nsa", "fox", "dil", "gates", "branch", "merge", "out",
                  "ln1", "wq", "psel", "pdense", "ln2"]
    if own_by_layer is None:
        own_by_layer = {0: (0, 1, 2, 3), 1: (1, 3)}

    stack = ExitStack()
    with stack:
        arena = stack.enter_context(nc.sbuf_tensor("arena", [128, 49152], F32))
        k.sb = SB(arena[:])
        ps01 = [stack.enter_context(nc.psum_tensor(f"ps{i}", [128, 512], F32))[:] for i in range(2)]
        accs = [stack.enter_context(nc.psum_tensor(f"acc{i}", [128, 2, 512], F32))[:] for i in range(2)]
        ps67 = [stack.enter_context(nc.psum_tensor(f"ps{i}", [128, 512], F32))[:] for i in (6, 7)]
        k.ACC = accs
        k.PS = ps01 + [accs[0][:, 0, :], accs[0][:, 1, :], accs[1][:, 0, :], accs[1][:, 1, :]] + ps67
        P = Prog(nc)
        k.P = P
        if "mod" in stages:
            stage_mod(k)
            P.barrier()
        for li, l in enumerate(layers):
            own = own_by_layer[l]
            xsrc = I["xctx"] if (li == 0 and "x2" not in k.inject) else S["x2"]
            last = (li == len(layers) - 1)
            if "modulate" in stages:
                stage_modulate(k, l, xsrc, 0)
                P.barrier()
            if "win" in stages:
                stage_win(k, l, own_tiles=own)
            for nm, fn in (("cmp", stage_cmp), ("cumsum", stage_cumsum)):
                if nm in stages:
                    fn(k, l)
                    P.barrier()
            for nm, fn in (("nsa", stage_nsa), ("fox", stage_fox), ("dil", stage_dil), ("gates", stage_gates),
                           ("branch", stage_branch), ("merge", stage_merge), ("out", stage_out)):
                if nm in stages:
                    fn(k, l, own)
                    P.barrier()
            if "ln1" in stages:
                stage_ln(k, l, own, 0, xsrc, S["y"], S["x1"], modulate=True)
                P.barrier()
            for nm, fn in (("wq", stage_wq), ("psel", stage_peer_sel), ("pdense", stage_peer_dense)):
                if nm in stages:
                    fn(k, l, own)
                    P.barrier()
            if "ln2" in stages:
                if last and final_out:
                    stage_ln(k, l, own, 1, S["x1"], S["y2"], k.out, dst_map={t: i for i, t in enumerate(own)},
                             modulate=False)
                else:
                    stage_ln(k, l, own, 1, S["x1"], S["y2"], S["x2"], modulate=False)
                P.barrier()
        P.emit(stack)
    k.n_instr = P.n_instr
    return k


def stage_mod(k):
    P, sb, PS, I, S = k.P, k.sb, k.PS, k.I, k.S
    sb.reset()
    cT = sb.f32(KC)
    NB = 256
    mrow = [sb.f32(NB, parts=1) for _ in range(2)]
    brow = [sb.f32(NB, parts=1) for _ in range(2)]
    wb = [sb.f32(KC * NB).rearrange("p (a b) -> p a b", b=NB) for _ in range(2)]
    P.dma("sp", cT, I["cT"][:, :], w=["cT"])
    for l in range(L):
        for cb in range(6 * D // NB):
            b = cb % 2
            P.dma("sp", brow[b], I["b_ada"][l:l + 1, cb * NB:(cb + 1) * NB], w=[("brow", b)])
            P.dma("sp", wb[b], I["w_ada"][l, :, cb * NB:(cb + 1) * NB].rearrange("(kc p) n -> p kc n", p=128),
                  w=[("wada", b)])
            ps = PS[b][0:1, 0:NB]
            for kc in range(KC):
                P.op("pe", lambda e, ps=ps, kc=kc, b=b: e.matmul(ps, cT[:, kc:kc + 1], wb[b][:, kc, :],
                                                               start=(kc == 0), stop=(kc == KC - 1)),
                     r=["cT", ("wada", b)], w=[("psm", b)])
            P.op("dve", lambda e, ps=ps, b=b: e.tensor_tensor(out=mrow[b], in0=ps, in1=brow[b], op=ALU.add),
                 r=[("psm", b), ("brow", b)], w=[("mrow", b)])
            P.dma("sp", S["mod"][l:l + 1, cb * NB:(cb + 1) * NB], mrow[b], r=[("mrow", b)], w=[])


def stage_modulate(k, l, xsrc, which):
    P, sb, PS, I, S = k.P, k.sb, k.PS, k.I, k.S
    sb.reset()
    ident = sb.f32(128)
    scp = sb.f32(D)
    sh = sb.f32(D)
    xt = [sb.f32(D) for _ in range(2)]
    ub = [sb.bf16(KC * 128).rearrange("p (a b) -> p a b", b=128) for _ in range(2)]
    o = which * 3 * D
    P.dma("sp", ident, I["ident"][:, :], w=["ident"])
    P.dma("sp", sh, S["mod"][l, o:o + D].partition_broadcast(128), w=["sh"])
    P.dma("sp", scp, S["mod"][l, o + D:o + 2 * D].partition_broadcast(128), w=["scp"])
    P.op("dve", lambda e: e.tensor_scalar(out=scp, in0=scp, scalar1=1.0, scalar2=None, op0=ALU.add),
         r=["scp"], w=["scp"])
    for tb in range(T // 128):
        b = tb % 2
        x = xt[b]
        P.dma("sp", x, xsrc[tb * 128:(tb + 1) * 128, :], w=[("x", b)])
        P.op("dve", lambda e, x=x: e.tensor_tensor(out=x, in0=x, in1=scp, op=ALU.mult),
             r=[("x", b), "scp"], w=[("x", b)])
        P.op("pool", lambda e, x=x: e.tensor_tensor(out=x, in0=x, in1=sh, op=ALU.add),
             r=[("x", b), "sh"], w=[("x", b)])
        for g in range(KC // 4):
            pb = g % 2
            for j in range(4):
                fc = g * 4 + j
                P.op("pe", lambda e, x=x, fc=fc, pb=pb, j=j: e.transpose(
                    out=PS[pb][:, j * 128:(j + 1) * 128], in_=x[:, fc * 128:(fc + 1) * 128], identity=ident),
                    r=[("x", b), "ident"], w=[("pst", pb)])
            dst = ub[b][:, g * 4:(g + 1) * 4, :]
            src = PS[pb][:, :].rearrange("p (a b) -> p a b", b=128)
            if g % 2 == 0:
                P.op("act", lambda e, dst=dst, src=src: e.copy(out=dst, in_=src), r=[("pst", pb)], w=[("ub", b)])
            else:
                P.op("dve", lambda e, dst=dst, src=src: e.tensor_copy(out=dst, in_=src), r=[("pst", pb)],
                     w=[("ub", b)])
        P.dma("sp", S["uT"][:, tb * 128:(tb + 1) * 128].rearrange("(kc p) t -> p kc t", p=128), ub[b],
              r=[("ub", b)], w=[])


def linear(k, xT, tiles, Kdim, wfn, segs, wq="pool"):
    P, sb, PS = k.P, k.sb, k.PS
    kc_n = Kdim // 128
    nt = len(tiles)
    XT = sb.bf16(kc_n * nt * 512).rearrange("p (a b) -> p a b", b=nt * 512)
    WB = [sb.bf16(kc_n * 512).rearrange("p (a b) -> p a b", b=512) for _ in range(2)]
    OB = [sb.f32(512) for _ in range(4)]
    for i, t in enumerate(tiles):
        P.dma("sp", XT[:, :, i * 512:(i + 1) * 512],
              xT[:, t * 512:(t + 1) * 512].rearrange("(kc p) t -> p kc t", p=128),
              r=["xT_dram_in"], w=[("XT", i)])
    k.lin_ctr = getattr(k, "lin_ctr", 0)
    for sg in segs:
        c0, c1 = sg["c0"], sg["c1"]
        odt = sg.get("odt", BF16)
        for cb0 in range(c0, c1, 512):
            ncb = min(512, c1 - cb0)
            wbi = k.lin_ctr % 2
            k.lin_ctr += 1
            W = WB[wbi]
            P.dma(wq, W[:, :, 0:ncb], wfn(cb0, cb0 + ncb).rearrange("(kc p) n -> p kc n", p=128),
                  w=[("WB", wbi)])
            sel = [i for i, t in enumerate(tiles) if t in sg["tiles"]]
            if sg["mode"] == "F":
                for s0 in range(0, ncb, 128):
                    m = min(128, ncb - s0)
                    for i in sel:
                        k.ps_ctr = getattr(k, "ps_ctr", 0) + 1
                        pi = k.ps_ctr % 4
                        ps = PS[pi][0:m, :]
                        for kc in range(kc_n):
                            P.op("pe", lambda e, ps=ps, W=W, kc=kc, s0=s0, m=m, i=i: e.matmul(
                                ps, W[:, kc, s0:s0 + m], XT[:, kc, i * 512:(i + 1) * 512],
                                start=(kc == 0), stop=(kc == kc_n - 1)),
                                r=[("WB", wbi), ("XT", i)], w=[("ps", pi)])
                        ob = OB[pi]
                        obv = (ob if odt == F32 else ob.bitcast(BF16)[:, 0:512])[0:m, :]
                        func = sg.get("func", AF.Copy)
                        scale = sg.get("scale", 1.0)
                        if sg.get("bias") is not None:
                            bcol = sg["bias"][0:m, (cb0 - c0 + s0) // 128:(cb0 - c0 + s0) // 128 + 1]
                            P.op("act", lambda e, obv=obv, ps=ps, func=func, scale=scale, bcol=bcol: e.activation(
                                out=obv, in_=ps, func=func, bias=bcol, scale=scale),
                                r=[("ps", pi), "bias"], w=[("ob", pi)])
                        elif func == AF.Copy and k.ps_ctr % 2 == 0:
                            P.op("dve", lambda e, obv=obv, ps=ps, scale=scale: e.tensor_scalar(
                                out=obv, in0=ps, scalar1=float(scale), scalar2=None, op0=ALU.mult),
                                r=[("ps", pi)], w=[("ob", pi)])
                        else:
                            P.op("act", lambda e, obv=obv, ps=ps, func=func, scale=scale: e.activation(
                                out=obv, in_=ps, func=func, scale=scale),
                                r=[("ps", pi)], w=[("ob", pi)])
                        t = tiles[i]
                        P.dma("sp", sg["out"][cb0 - c0 + s0:cb0 - c0 + s0 + m, t * 512:(t + 1) * 512], obv,
                              r=[("ob", pi)], w=[])
            else:
                for i in sel:
                    for tb in range(4):
                        k.ps_ctr = getattr(k, "ps_ctr", 0) + 1
                        pi = k.ps_ctr % 4
                        ps = PS[pi][:, 0:ncb]
                        for kc in range(kc_n):
                            P.op("pe", lambda e, ps=ps, W=W, kc=kc, i=i, tb=tb, ncb=ncb: e.matmul(
                                ps, XT[:, kc, i * 512 + tb * 128:i * 512 + (tb + 1) * 128], W[:, kc, 0:ncb],
                                start=(kc == 0), stop=(kc == kc_n - 1)),
                                r=[("WB", wbi), ("XT", i)], w=[("ps", pi)])
                        ob = OB[pi]
                        obv = (ob if odt == F32 else ob.bitcast(BF16)[:, 0:512])[:, 0:ncb]
                        func = sg.get("func", AF.Copy)
                        if func == AF.Copy and k.ps_ctr % 2 == 0:
                            P.op("dve", lambda e, obv=obv, ps=ps: e.tensor_copy(out=obv, in_=ps),
                                 r=[("ps", pi)], w=[("ob", pi)])
                        else:
                            P.op("act", lambda e, obv=obv, ps=ps, func=func: e.activation(out=obv, in_=ps, func=func),
                                 r=[("ps", pi)], w=[("ob", pi)])
                        t = tiles[i]
                        r0 = t * 512 + tb * 128
                        P.dma("sp", sg["out"][r0:r0 + 128, cb0 - c0:cb0 - c0 + ncb], obv,
                              r=[("ob", pi)], w=[])


def stage_win(k, l, own_tiles):
    P, sb, I, S = k.P, k.sb, k.I, k.S
    for grp in ((0, 1), (2, 3)):
        sb.reset()
        segs = []
        for name, (c0, c1) in SEG.items():
            tl = [t for t in grp if (name not in Q_SEGS or t in own_tiles)]
            if not tl:
                continue
            sg = dict(name=name, c0=c0, c1=c1, mode="F" if name in F_SEGS else "T", out=S[name], tiles=tl)
            if name in ("qa", "foxq", "dilq"):
                sg["scale"] = HD ** -0.5
            if name in ("gates", "forget"):
                sg["odt"] = F32
            if name == "gates":
                sg["func"] = AF.Sigmoid
            segs.append(sg)
        linear(k, S["uT"], grp, D, lambda c0, c1: I["w_in"][l, :, c0:c1], segs)
        P.barrier()


TAB = dict(Dc=(0, 2432), D511=(2432, 1408), D128=(3840, 1024), Ddil4=(4864, 1408), Ddil16=(6272, 2432),
           Fox01=(8704, 2432))
TABW = 11136
HUGE = 1.0e9
MASKV = -30000.0


def alibi_slopes(n=24):
    return (np.float32(2.0) ** (-8.0 * np.arange(1, n + 1, dtype=np.float32) / n)).astype(np.float32)


def host_tables(h):
    shift = 512 if h == 0 else 0
    p = np.arange(128)[:, None]
    tabs = np.zeros((128, TABW), np.float32)

    def fill(name, fn):
        o, w = TAB[name]
        r = np.arange(w)[None, :] - 384
        dist = r - p
        tabs[:, o:o + w] = fn(dist)
    fill("Dc", lambda d: np.where(d >= 0, d, HUGE))
    fill("D511", lambda d: np.where((d >= 0) & (d <= 511), d, HUGE))
    fill("D128", lambda d: np.where((d >= 0) & (d <= 128), d, HUGE))
    fill("Ddil4", lambda d: np.where((d >= 0) & (d <= 512) & (d % 4 == 0), d, HUGE))
    fill("Ddil16", lambda d: np.where((d >= 0) & (d % 16 == 0), d, HUGE))
    fill("Fox01", lambda d: np.where(d >= 0, 0.0, MASKV))
    kbias = np.zeros((128, 16), np.float32)
    kbias[:, :shift // 128] = MASKV
    n = np.arange(128)[:, None]
    q = np.arange(T)[None, :]
    dc = q - (16 * n + 31)
    okc = (dc >= 0) & (16 * n >= shift) & (n <= 126)
    dcmp = np.where(okc, dc, HUGE).astype(np.float32)
    cm = np.zeros((128, 32), np.float32)
    for j in range(32):
        for m in range(4):
            for nn in range(2):
                i = 4 * j + m + nn
                if i < 127:
                    cm[i, j] += 1.0
    E = np.zeros((32, T), np.float32)
    E[np.arange(T) // 64, np.arange(T)] = 1.0
    g = np.arange(T) - shift
    jb = np.arange(32)[None, :] - shift // 64
    qblk = (g // 64)[:, None]
    forced = ((jb == 0) | (jb == qblk) | (jb == qblk - 1)) & (g[:, None] >= 0)
    causal = (jb >= 0) & (jb <= qblk) & (g[:, None] >= 0)
    smul = (causal & ~forced).astype(np.float32)
    sadd = np.where(forced, 1e30, np.where(causal, 0.0, -1e30)).astype(np.float32)
    tri = np.zeros((128, 5, 512), np.float32)
    s = np.arange(128)[:, None]
    t = np.arange(512)[None, :]
    for d in range(4):
        tri[:, d, :] = (128 * d + s <= t)
    tri[:, 4, :] = 1.0
    import ml_dtypes
    return dict(tabs=tabs, kbias=kbias, dcmp=dcmp, cm=cm, Emat=E.astype(ml_dtypes.bfloat16), smul=smul, sadd=sadd,
                tri=tri, ident=np.eye(128, dtype=np.float32))


def bc_ap(ap, axis, n):
    lay = [list(x) for x in ap.ap]
    lay.insert(1 + axis, [0, n])
    return bass.AP(ap.tensor, ap.offset, lay)


class Attn:
    def __init__(self, k, own_tiles):
        self.k = k
        self.own = own_tiles
        P, sb, PS, I, S = k.P, k.sb, k.PS, k.I, k.S
        sb.reset()
        self.tabs = sb.f32(TABW)
        self.kbias = sb.f32(16)
        self.ident = sb.f32(128)
        P.dma("sp", self.tabs, I["tabs"][:, :], w=["tabs"])
        P.dma("sp", self.kbias, I["kbias"][:, :], w=["kbias"])
        P.dma("sp", self.ident, I["ident"][:, :], w=["ident"])
        self.KT = [sb.bf16(T) for _ in range(2)]
        self.V = [sb.bf16(16 * 129).rearrange("p (a b) -> p a b", b=129) for _ in range(2)]
        for i in range(2):
            P.op("pool", lambda e, v=self.V[i]: e.memset(v, 1.0), w=[("V", i)])
        self.QT = [sb.bf16(512) for _ in range(2)]
        self.Ssb = [sb.f32(512) for _ in range(3)]
        self.PT = [sb.bf16(512) for _ in range(3)]
        self.kv_ctr = 0
        self.q_ctr = 0
        self.s_ctr = 0
        self.acc_ctr = 0
        self.sm = sb.f32(64)
        self.sm_ctr = 0
        self.ACC = k.ACC

    def load_kv(self, kt_dram_rows, v_dram_cols):
        P = self.k.P
        i = self.kv_ctr % 2
        self.kv_ctr += 1
        P.dma("sp", self.KT[i], kt_dram_rows, w=[("KT", i)])
        if v_dram_cols is not None:
            P.dma("sp", self.V[i][:, :, 0:128], v_dram_cols.rearrange("(b p) d -> p b d", p=128), w=[("V", i)])
        return i

    def load_q(self, q_dram):
        P = self.k.P
        i = self.q_ctr % 2
        self.q_ctr += 1
        P.dma("sp", self.QT[i], q_dram, w=[("QT", i)])
        return i

    def tiles(self, qi, kvi, tl, kblks, tabname, scalar, aug=None, aug_keys=()):
        k, P, PS = self.k, self.k.P, self.k.PS
        ai = self.acc_ctr % 2
        self.acc_ctr += 1
        acc = self.ACC[ai]
        o, w = TAB[tabname]
        first = True
        for n_i, kb in enumerate(kblks):
            si = self.s_ctr % 2
            bi = self.s_ctr % 3
            self.s_ctr += 1
            ps = PS[si]
            KTs = self.KT[kvi][:, kb * 128:(kb + 1) * 128]
            QTt = self.QT[qi]
            P.op("pe", lambda e, ps=ps, KTs=KTs, QTt=QTt, aug=aug: e.matmul(ps, KTs, QTt, start=True,
                                                                           stop=(aug is None)),
                 r=[("KT", kvi), ("QT", qi)], w=[("S", si)])
            if aug is not None:
                lhs, rhs = aug(kb)
                P.op("pe", lambda e, ps=ps, lhs=lhs, rhs=rhs: e.matmul(ps, lhs, rhs, start=False, stop=True),
                     r=list(aug_keys), w=[("S", si)])
            c0 = o + 512 * tl - 128 * kb + 384
            assert c0 >= o and c0 + 512 <= o + w, (tabname, tl, kb)
            tsl = self.tabs[:, c0:c0 + 512]
            ssb = self.Ssb[bi]
            P.op("dve", lambda e, ssb=ssb, tsl=tsl, ps=ps, scalar=scalar: e.scalar_tensor_tensor(
                out=ssb, in0=tsl, scalar=float(scalar), in1=ps, op0=ALU.mult, op1=ALU.add),
                r=[("S", si), "tabs"], w=[("Ssb", bi)])
            pt = self.PT[bi]
            kbc = self.kbias[:, kb:kb + 1]
            P.op("act", lambda e, pt=pt, ssb=ssb, kbc=kbc: e.activation(out=pt, in_=ssb, func=AF.Exp, bias=kbc),
                 r=[("Ssb", bi), "kbias"], w=[("PT", bi)])
            Vb = self.V[kvi][:, kb, :]
            for qb in range(4):
                dst = acc[:, qb // 2, (qb % 2) * 129:(qb % 2) * 129 + 129]
                st = first and (qb % 2 == 0)
                P.op("pe", lambda e, dst=dst, pt=pt, qb=qb, Vb=Vb, st=st: e.matmul(
                    dst, pt[:, qb * 128:(qb + 1) * 128], Vb, start=st, stop=True, skip_group_check=True),
                    r=[("PT", bi), ("V", kvi)], w=[("ACC", ai)])
            first = False
        return ai

    def rinv(self, ai, ncols=129, gate=None):
        P = self.k.P
        acc = self.ACC[ai]
        j = self.sm_ctr % 8
        self.sm_ctr += 1
        r = self.sm[:, j * 4:(j + 1) * 4]
        rs = acc[:, :, 0:2 * ncols].rearrange("p b (q c) -> p b q c", c=ncols)[:, :, :, ncols - 1]
        r3 = r.rearrange("p (b q) -> p b q", q=2)
        P.op("dve", lambda e, r3=r3, rs=rs: e.tensor_scalar(out=r3, in0=rs, scalar1=1e-30, scalar2=None, op0=ALU.max),
             r=[("ACC", ai)], w=[("sm", j)])
        P.op("dve", lambda e, r=r: e.reciprocal(out=r, in_=r), r=[("sm", j)], w=[("sm", j)])
        if gate is not None:
            P.op("dve", lambda e, r=r, gate=gate: e.tensor_tensor(out=r, in0=r, in1=gate, op=ALU.mult),
                 r=[("sm", j), "gates"], w=[("sm", j)])
        return r, ("sm", j)

    def scale_into(self, ai, r, rkey, dst, dkey, accumulate, ncols=129):
        P = self.k.P
        acc = self.ACC[ai]
        for qb in range(4):
            src = acc[:, qb // 2, (qb % 2) * ncols:(qb % 2) * ncols + 128]
            d = dst[:, qb, :]
            rc = r[:, qb:qb + 1]
            if accumulate:
                P.op("dve", lambda e, d=d, src=src, rc=rc: e.scalar_tensor_tensor(
                    out=d, in0=src, scalar=rc, in1=d, op0=ALU.mult, op1=ALU.add),
                    r=[("ACC", ai), rkey, dkey], w=[dkey])
            else:
                P.op("dve", lambda e, d=d, src=src, rc=rc: e.tensor_scalar(
                    out=d, in0=src, scalar1=rc, scalar2=None, op0=ALU.mult),
                    r=[("ACC", ai), rkey], w=[dkey])

    def store_oT(self, src, skey, row0, tl, obuf, okey):
        k, P, PS = self.k, self.k.P, self.k.PS
        tb = 6 + (self.acc_ctr % 2)
        self.acc_ctr += 0
        ps = PS[tb]
        for qb in range(4):
            P.op("pe", lambda e, ps=ps, src=src, qb=qb: e.transpose(out=ps[:, qb * 128:(qb + 1) * 128],
                                                                   in_=src[:, qb, :], identity=self.ident),
                 r=[skey, "ident"], w=[("pst", tb)])
        P.op("act", lambda e, obuf=obuf, ps=ps: e.copy(out=obuf, in_=ps), r=[("pst", tb)], w=[okey])
        P.dma("sp", k.S["oT"][row0:row0 + 128, tl * 512:(tl + 1) * 512], obuf, r=[okey], w=[])


GELU = AF.Gelu_apprx_tanh


def stage_cmp(k, l):
    P, sb, PS, I, S = k.P, k.sb, k.PS, k.I, k.S
    sb.reset()
    NBK = 127
    for i, (src, ) in enumerate((("kcmp",), ("vcmp",))):
        W1 = sb.bf16(32 * 128).rearrange("p (a b) -> p a b", b=128)
        W2 = sb.bf16(128)
        peT = sb.bf16(32)
        hb = sb.f32(1)
        P.dma("pool", W1, I["cmp_w1"][l, i].rearrange("(a p) o -> p a o", p=128), w=[("W1", i)])
        P.dma("pool", W2, I["cmp_w2"][l, i], w=[("W2", i)])
        P.dma("pool", peT, I["cmp_peT"][l, i], w=[("peT", i)])
        psb = PS[0][:, 0:1]
        for a in range(32):
            P.op("pe", lambda e, a=a, W1=W1, peT=peT, psb=psb: e.matmul(psb, W1[:, a, :], peT[:, a:a + 1],
                                                                       start=(a == 0), stop=(a == 31)),
                 r=[("W1", i), ("peT", i)], w=[("psb", i)])
        P.op("dve", lambda e, hb=hb, psb=psb: e.tensor_copy(out=hb, in_=psb), r=[("psb", i)], w=[("hb", i)])
        for kv in range(3):
            aT = sb.bf16(T)
            hm = sb.bf16(128)
            P.dma("sp", aT, S[src][kv * 128:(kv + 1) * 128, :], w=[("aT", i, kv)])
            P.op("pool", lambda e, hm=hm: e.memset(hm, 0.0), w=[("hm", i, kv)])
            ps = PS[1 + (kv % 2)][:, 0:NBK]
            for a in range(32):
                rhs = aT[:, a:a + 16 * (NBK - 1) + 1:16]
                P.op("pe", lambda e, ps=ps, a=a, W1=W1, rhs=rhs: e.matmul(ps, W1[:, a, :], rhs, start=(a == 0),
                                                                         stop=(a == 31)),
                     r=[("W1", i), ("aT", i, kv)], w=[("psh", kv % 2)])
            P.op("act", lambda e, hm=hm, ps=ps, hb=hb: e.activation(out=hm[:, 0:NBK], in_=ps, func=GELU, bias=hb),
                 r=[("psh", kv % 2), ("hb", i), ("hm", i, kv)], w=[("hm", i, kv)])
            po = PS[3 + (kv % 2)][:, 0:128]
            if i == 0:
                P.op("pe", lambda e, po=po, W2=W2, hm=hm: e.matmul(po, W2, hm, start=True, stop=True),
                     r=[("W2", i), ("hm", i, kv)], w=[("pso", kv % 2)])
                ob = sb.bf16(128)
                P.op("dve", lambda e, ob=ob, po=po: e.tensor_copy(out=ob, in_=po), r=[("pso", kv % 2)],
                     w=[("cob", i, kv)])
                P.dma("sp", S["kc"][kv], ob, r=[("cob", i, kv)], w=[])
            else:
                P.op("pe", lambda e, po=po, W2=W2, hm=hm: e.matmul(po, hm, W2, start=True, stop=True),
                     r=[("W2", i), ("hm", i, kv)], w=[("pso", kv % 2)])
                ob = sb.f32(128)
                P.op("dve", lambda e, ob=ob, po=po: e.tensor_copy(out=ob, in_=po), r=[("pso", kv % 2)],
                     w=[("cob", i, kv)])
                P.dma("sp", S["vc"][kv], ob, r=[("cob", i, kv)], w=[])


def stage_cumsum(k, l):
    P, sb, PS, I, S = k.P, k.sb, k.PS, k.I, k.S
    sb.reset()
    ft = sb.f32(T, parts=8)
    nF = sb.f32(T, parts=8)
    Fr = sb.f32(T, parts=8)
    ones = sb.f32(T, parts=8)
    nb = sb.f32(1, parts=8)
    P.dma("sp", ft, S["forget"], w=["ft"])
    P.dma("sp", nb, I["b_forget"][l, :].rearrange("(h o) -> h o", o=1), w=["nb"])
    P.op("dve", lambda e: e.tensor_scalar(out=nb, in0=nb, scalar1=-1.0, scalar2=None, op0=ALU.mult), r=["nb"], w=["nb"])
    P.op("pool", lambda e: e.memset(ones, 1.0), w=["ones"])
    P.op("act", lambda e: e.activation(out=ft, in_=ft, func=AF.Exp, scale=-1.0, bias=nb), r=["ft", "nb"], w=["ft"])
    P.op("dve", lambda e: e.tensor_scalar(out=ft, in0=ft, scalar1=1.0, scalar2=None, op0=ALU.add), r=["ft"], w=["ft"])
    P.op("act", lambda e: e.activation(out=ft, in_=ft, func=AF.Ln), r=["ft"], w=["ft"])
    P.op("dve", lambda e: e.tensor_tensor_scan(out=nF, data0=ones, data1=ft, initial=0.0, op0=ALU.mult, op1=ALU.add),
         r=["ft", "ones"], w=["nF"])
    P.op("dve", lambda e: e.tensor_scalar(out=Fr, in0=nF, scalar1=-1.0, scalar2=None, op0=ALU.mult), r=["nF"], w=["Fr"])
    P.dma("sp", S["Frow"], Fr, r=["Fr"], w=[])
    P.dma("sp", S["nFrow"], nF, r=["nF"], w=[])


def stage_nsa(k, l, own):
    P, sb, PS, I, S = k.P, k.sb, k.PS, k.I, k.S
    A = Attn(k, own)
    sl = alibi_slopes()
    dcmp = sb.f32(T)
    P.dma("sp", dcmp, I["dcmp"], w=["dcmp"])
    Esb = sb.bf16(T, parts=32)
    P.dma("sp", Esb, I["Emat"], w=["E"])
    rhs161 = [sb.f32(161) for _ in range(2)]
    kcT = [sb.bf16(128) for _ in range(2)]
    Ef = [sb.f32(512) for _ in range(2)]
    ACCW = [PS[2 + 2 * i] for i in range(2)]
    oacc = {(hh, tl): sb.f32(512).rearrange("p (q d) -> p q d", d=128) for hh in range(4) for tl in own}
    pslc = {tl: sb.f32(128).rearrange("p (q j) -> p q j", j=32) for tl in own}
    gat = {tl: sb.f32(4 * 36).rearrange("p (q c) -> p q c", c=36) for tl in own}
    smul = {tl: sb.f32(128).rearrange("p (q j) -> p q j", j=32) for tl in own}
    sadd = {tl: sb.f32(128).rearrange("p (q j) -> p q j", j=32) for tl in own}
    selT = {tl: sb.bf16(512, parts=32) for tl in own}
    sc = sb.f32(128).rearrange("p (q j) -> p q j", j=32)
    sc2 = sb.f32(128).rearrange("p (q j) -> p q j", j=32)
    m8 = sb.f32(64).rearrange("p (q j) -> p q j", j=16)
    obuf = [sb.bf16(512) for _ in range(2)]
    for tl in own:
        tsl = slice(tl * 512, (tl + 1) * 512)
        P.dma("sp", gat[tl], S["gates"][tsl, :].rearrange("(q p) c -> p q c", p=128), w=["gates"])
        P.dma("sp", smul[tl], I["smul"][tsl, :].rearrange("(q p) c -> p q c", p=128), w=[("smul", tl)])
        P.dma("sp", sadd[tl], I["sadd"][tsl, :].rearrange("(q p) c -> p q c", p=128), w=[("sadd", tl)])
    cmpc = 0
    for kv in range(3):
        b = kv % 2
        P.op("pool", lambda e, b=b: e.memset(rhs161[b], 1.0), w=[("rhs161", b)])
        P.dma("sp", rhs161[b][:, 0:128], S["vc"][kv], w=[("rhs161", b)])
        P.dma("sp", rhs161[b][:, 128:160], I["cm"], w=[("rhs161", b)])
        P.dma("sp", kcT[b], S["kc"][kv], w=[("kcT", b)])
        for hh in range(4):
            hq = kv * 4 + hh
            slope = float(sl[2 * hq])
            for tl in own:
                qi = A.load_q(S["qa"][hq * 128:(hq + 1) * 128, tl * 512:(tl + 1) * 512])
                si = A.s_ctr % 2
                bi = A.s_ctr % 3
                A.s_ctr += 1
                ps = PS[si]
                P.op("pe", lambda e, ps=ps, b=b, qi=qi: e.matmul(ps, kcT[b], A.QT[qi], start=True, stop=True),
                     r=[("kcT", b), ("QT", qi)], w=[("S", si)])
                ssb = A.Ssb[bi]
                dsl = dcmp[:, tl * 512:(tl + 1) * 512]
                P.op("dve", lambda e, ssb=ssb, dsl=dsl, ps=ps, slope=slope: e.scalar_tensor_tensor(
                    out=ssb, in0=dsl, scalar=-slope, in1=ps, op0=ALU.mult, op1=ALU.add),
                    r=[("S", si), "dcmp"], w=[("Ssb", bi)])
                ef = Ef[cmpc % 2]
                ek = ("Ef", cmpc % 2)
                cmpc += 1
                P.op("act", lambda e, ef=ef, ssb=ssb: e.activation(out=ef, in_=ssb, func=AF.Exp),
                     r=[("Ssb", bi)], w=[ek])
                ai = A.acc_ctr % 2
                A.acc_ctr += 1
                acc = A.ACC[ai]
                for qb in range(4):
                    dst = acc[:, qb // 2, (qb % 2) * 161:(qb % 2) * 161 + 161]
                    P.op("pe", lambda e, dst=dst, ef=ef, qb=qb, b=b: e.matmul(
                        dst, ef[:, qb * 128:(qb + 1) * 128], rhs161[b], start=(qb % 2 == 0), stop=True,
                        skip_group_check=True), r=[ek, ("rhs161", b)], w=[("ACC", ai)])
                r0, r0k = A.rinv(ai, ncols=161)
                for qb in range(4):
                    src = acc[:, qb // 2, (qb % 2) * 161 + 128:(qb % 2) * 161 + 160]
                    d = pslc[tl][:, qb, :]
                    rc = r0[:, qb:qb + 1]
                    if hh == 0:
                        P.op("dve", lambda e, d=d, src=src, rc=rc: e.tensor_scalar(
                            out=d, in0=src, scalar1=rc, scalar2=None, op0=ALU.mult),
                            r=[("ACC", ai), r0k], w=[("pslc", tl)])
                    else:
                        P.op("dve", lambda e, d=d, src=src, rc=rc: e.scalar_tensor_tensor(
                            out=d, in0=src, scalar=rc, in1=d, op0=ALU.mult, op1=ALU.add),
                            r=[("ACC", ai), r0k, ("pslc", tl)], w=[("pslc", tl)])
                g0 = gat[tl][:, :, hq * 3 + 0]
                P.op("dve", lambda e, r0=r0, g0=g0: e.tensor_tensor(out=r0, in0=r0, in1=g0, op=ALU.mult),
                     r=[r0k, "gates"], w=[r0k])
                A.scale_into(ai, r0, r0k, oacc[(hh, tl)], ("oacc", hh, tl), accumulate=False, ncols=161)
        for tl in own:
            P.op("dve", lambda e, tl=tl: e.tensor_tensor(out=sc, in0=pslc[tl], in1=smul[tl], op=ALU.mult),
                 r=[("pslc", tl), ("smul", tl)], w=["sc"])
            P.op("dve", lambda e, tl=tl: e.tensor_tensor(out=sc, in0=sc, in1=sadd[tl], op=ALU.add),
                 r=["sc", ("sadd", tl)], w=["sc"])
            for qb in range(4):
                P.op("dve", lambda e, qb=qb: e.max(out=m8[:, qb, 0:8], in_=sc[:, qb, :]), r=["sc"], w=["m8"])
                P.op("dve", lambda e, qb=qb: e.match_replace(out=sc2[:, qb, :], in_to_replace=m8[:, qb, 0:8],
                                                             in_values=sc[:, qb, :], imm_value=-3.0e38),
                     r=["sc", "m8"], w=["sc2"])
                P.op("dve", lambda e, qb=qb: e.max(out=m8[:, qb, 8:16], in_=sc2[:, qb, :]), r=["sc2"], w=["m8"])
                P.op("dve", lambda e, qb=qb: e.tensor_scalar(out=sc2[:, qb, :], in0=sc[:, qb, :],
                                                             scalar1=m8[:, qb, 15:16], scalar2=None, op0=ALU.is_ge),
                     r=["sc", "m8", "sc2"], w=["sc2"])
            P.op("dve", lambda e: e.tensor_scalar(out=sc2, in0=sc2, scalar1=-1.0, scalar2=-MASKV, op0=ALU.add,
                                                  op1=ALU.mult), r=["sc2"], w=["sc2"])
            pst = PS[6][0:32, :]
            for qb in range(4):
                P.op("pe", lambda e, qb=qb, pst=pst: e.transpose(out=pst[:, qb * 128:(qb + 1) * 128],
                                                                in_=sc2[:, qb, :], identity=A.ident),
                     r=["sc2", "ident"], w=[("pst", 6)])
            P.op("act", lambda e, tl=tl, pst=pst: e.copy(out=selT[tl], in_=pst), r=[("pst", 6)], w=[("selT", tl)])
        for br, (ks, vs, tabn) in enumerate((("kslc", "vslc", "Dc"), ("kwin", "vwin", "D511"))):
            kvi = A.load_kv(S[ks][kv * 128:(kv + 1) * 128, :], S[vs][:, kv * 128:(kv + 1) * 128])
            for hh in range(4):
                hq = kv * 4 + hh
                slope = float(sl[2 * hq])
                for tl in own:
                    qi = A.load_q(S["qa"][hq * 128:(hq + 1) * 128, tl * 512:(tl + 1) * 512])
                    if br == 0:
                        kbl = list(range(0, 4 * tl + 4))
                        aug = (lambda kb, tl=tl: (Esb[:, kb * 128:(kb + 1) * 128], selT[tl]))
                        ai = A.tiles(qi, kvi, tl, kbl, tabn, -slope, aug=aug, aug_keys=["E", ("selT", tl)])
                    else:
                        kbl = list(range(max(0, 4 * tl - 4), 4 * tl + 4))
                        ai = A.tiles(qi, kvi, tl, kbl, tabn, -slope)
                    r, rk = A.rinv(ai, gate=gat[tl][:, :, hq * 3 + 1 + br])
                    A.scale_into(ai, r, rk, oacc[(hh, tl)], ("oacc", hh, tl), accumulate=True)
        for hh in range(4):
            hq = kv * 4 + hh
            for tl in own:
                ob = obuf[(hh + tl) % 2]
                A.store_oT(oacc[(hh, tl)], ("oacc", hh, tl), hq * 128, tl, ob, ("obuf", (hh + tl) % 2))


def stage_fox(k, l, own):
    P, sb, PS, I, S = k.P, k.sb, k.PS, k.I, k.S
    A = Attn(k, own)
    FAq = [sb.f32(T, parts=2) for _ in range(2)]
    FAk = [sb.f32(T, parts=2) for _ in range(2)]
    ob32 = [sb.f32(512).rearrange("p (q d) -> p q d", d=128) for _ in range(2)]
    obuf = [sb.bf16(512) for _ in range(2)]
    for i in range(2):
        P.op("pool", lambda e, i=i: e.memset(FAq[i], 1.0), w=[("FA", i)])
        P.op("pool", lambda e, i=i: e.memset(FAk[i], 1.0), w=[("FA", i)])
    c = 0
    for hd in range(8):
        kvi = A.load_kv(S["foxk"][hd * 128:(hd + 1) * 128, :], S["foxv"][:, hd * 128:(hd + 1) * 128])
        fb = hd % 2
        P.dma("sp", FAq[fb][0:1, :], S["Frow"][hd:hd + 1, :], w=[("FA", fb)])
        P.dma("sp", FAk[fb][1:2, :], S["nFrow"][hd:hd + 1, :], w=[("FA", fb)])
        for tl in own:
            qi = A.load_q(S["foxq"][hd * 128:(hd + 1) * 128, tl * 512:(tl + 1) * 512])
            aug = (lambda kb, tl=tl, fb=fb: (FAk[fb][:, kb * 128:(kb + 1) * 128], FAq[fb][:, tl * 512:(tl + 1) * 512]))
            ai = A.tiles(qi, kvi, tl, list(range(0, 4 * tl + 4)), "Fox01", 1.0, aug=aug, aug_keys=[("FA", fb)])
            r, rk = A.rinv(ai)
            ob = ob32[c % 2]
            A.scale_into(ai, r, rk, ob, ("ob32", c % 2), accumulate=False)
            A.store_oT(ob, ("ob32", c % 2), 1536 + hd * 128, tl, obuf[c % 2], ("obuf", c % 2))
            c += 1


def stage_dil(k, l, own):
    P, sb, PS, I, S = k.P, k.sb, k.PS, k.I, k.S
    A = Attn(k, own)
    sl = alibi_slopes()
    oraw = {tl: sb.f32(4 * 129).rearrange("p (q d) -> p q d", d=129) for tl in own}
    ob32 = [sb.f32(512).rearrange("p (q d) -> p q d", d=128) for _ in range(2)]
    obuf = [sb.bf16(512) for _ in range(2)]
    rr = sb.f32(8)
    tabn = ["D128", "Ddil4", "Ddil16"]
    c = 0
    for j in range(4):
        for g in range(3):
            hx = g * 4 + j
            slope = float(sl[2 * hx + 1])
            kvi = A.load_kv(S["dilk"][hx * 128:(hx + 1) * 128, :], S["dilv"][:, hx * 128:(hx + 1) * 128])
            for tl in own:
                qi = A.load_q(S["dilq"][hx * 128:(hx + 1) * 128, tl * 512:(tl + 1) * 512])
                if g == 0:
                    kbl = list(range(max(0, 4 * tl - 1), 4 * tl + 4))
                elif g == 1:
                    kbl = list(range(max(0, 4 * tl - 4), 4 * tl + 4))
                else:
                    kbl = list(range(0, 4 * tl + 4))
                ai = A.tiles(qi, kvi, tl, kbl, tabn[g], -slope)
                acc = A.ACC[ai]
                for bk in range(2):
                    src = acc[:, bk, 0:258]
                    d = oraw[tl][:, 2 * bk:2 * bk + 2, :].rearrange("p q d -> p (q d)")
                    if g == 0:
                        P.op("dve", lambda e, d=d, src=src: e.tensor_copy(out=d, in_=src), r=[("ACC", ai)],
                             w=[("oraw", tl)])
                    else:
                        P.op("dve", lambda e, d=d, src=src: e.tensor_tensor(out=d, in0=d, in1=src, op=ALU.add),
                             r=[("ACC", ai), ("oraw", tl)], w=[("oraw", tl)])
        for tl in own:
            r = rr[:, (c % 2) * 4:(c % 2) * 4 + 4]
            rk = ("rr", c % 2)
            P.op("dve", lambda e, r=r, tl=tl: e.tensor_scalar(out=r, in0=oraw[tl][:, :, 128], scalar1=1e-30,
                                                              scalar2=None, op0=ALU.max), r=[("oraw", tl)], w=[rk])
            P.op("dve", lambda e, r=r: e.reciprocal(out=r, in_=r), r=[rk], w=[rk])
            ob = ob32[c % 2]
            for qb in range(4):
                P.op("dve", lambda e, ob=ob, tl=tl, qb=qb, r=r: e.tensor_scalar(
                    out=ob[:, qb, :], in0=oraw[tl][:, qb, 0:128], scalar1=r[:, qb:qb + 1], scalar2=None,
                    op0=ALU.mult), r=[("oraw", tl), rk], w=[("ob32", c % 2)])
            A.store_oT(ob, ("ob32", c % 2), 2560 + j * 128, tl, obuf[c % 2], ("obuf", c % 2))
            c += 1


ALPHA = (2.0 * L) ** 0.25
LN_EPS = 1e-5


def own_groups(own):
    own = list(own)
    return [tuple(own[i:i + 2]) for i in range(0, len(own), 2)]


def stage_gates(k, l, own):
    P, sb, I, S = k.P, k.sb, k.I, k.S
    for grp in own_groups(own):
        sb.reset()
        bg = sb.f32(96)
        P.dma("sp", bg, I["b_gateT"][l], w=["bias"])
        sg = dict(name="g", c0=0, c1=3 * D, mode="F", out=S["gT"], tiles=list(grp), func=AF.Sigmoid, bias=bg)
        linear(k, S["uT"], grp, D, lambda c0, c1: I["w_gate"][l, :, c0:c1], [sg])
        P.barrier()


def stage_branch(k, l, own):
    P, sb, I, S = k.P, k.sb, k.I, k.S
    for grp in own_groups(own):
        for nm, r0, r1 in (("ya", 0, 1536), ("yb", 1536, 2560), ("yc", 2560, 3072)):
            sb.reset()
            sg = dict(name=nm, c0=0, c1=D, mode="F", out=S[nm], tiles=list(grp))
            linear(k, S["oT"][r0:r1, :], grp, r1 - r0, lambda c0, c1, r0=r0, r1=r1: I["w_branch"][l, r0:r1, c0:c1],
                   [sg])
            P.barrier()


def stage_merge(k, l, own):
    P, sb, I, S = k.P, k.sb, k.I, k.S
    sb.reset()
    NB = 3
    bufs = [[sb.bf16(512) for _ in range(6)] for _ in range(NB)]
    tmp = [[sb.f32(512) for _ in range(2)] for _ in range(NB)]
    outb = [sb.bf16(512) for _ in range(NB)]
    c = 0
    for fc in range(KC):
        for tl in own:
            b = c % NB
            c += 1
            ts = slice(tl * 512, (tl + 1) * 512)
            srcs = [S["gT"][fc * 128:(fc + 1) * 128, ts], S["gT"][D + fc * 128:D + (fc + 1) * 128, ts],
                    S["gT"][2 * D + fc * 128:2 * D + (fc + 1) * 128, ts],
                    S["ya"][fc * 128:(fc + 1) * 128, ts], S["yb"][fc * 128:(fc + 1) * 128, ts],
                    S["yc"][fc * 128:(fc + 1) * 128, ts]]
            for j in range(6):
                P.dma("sp" if j % 2 == 0 else "act", bufs[b][j], srcs[j], w=[("mb", b, j)])
            B = bufs[b]
            t0, t1 = tmp[b]
            P.op("dve", lambda e, B=B, t0=t0: e.tensor_tensor(out=t0, in0=B[0], in1=B[3], op=ALU.mult),
                 r=[("mb", b, 0), ("mb", b, 3)], w=[("mt0", b)])
            P.op("pool", lambda e, B=B, t1=t1: e.tensor_tensor(out=t1, in0=B[1], in1=B[4], op=ALU.mult),
                 r=[("mb", b, 1), ("mb", b, 4)], w=[("mt1", b)])
            P.op("dve", lambda e, t0=t0, t1=t1: e.tensor_tensor(out=t0, in0=t0, in1=t1, op=ALU.add),
                 r=[("mt0", b), ("mt1", b)], w=[("mt0", b)])
            P.op("pool", lambda e, B=B, t1=t1: e.tensor_tensor(out=t1, in0=B[2], in1=B[5], op=ALU.mult),
                 r=[("mb", b, 2), ("mb", b, 5), ("mt1", b)], w=[("mt1", b)])
            ob = outb[b]
            P.op("dve", lambda e, t0=t0, t1=t1, ob=ob: e.tensor_tensor(out=ob, in0=t0, in1=t1, op=ALU.add),
                 r=[("mt0", b), ("mt1", b)], w=[("mob", b)])
            P.dma("sp", S["mT"][fc * 128:(fc + 1) * 128, ts], ob, r=[("mob", b)], w=[])


def stage_out(k, l, own):
    P, sb, I, S = k.P, k.sb, k.I, k.S
    for grp in own_groups(own):
        sb.reset()
        sg = dict(name="y", c0=0, c1=D, mode="T", out=S["y"], tiles=list(grp), odt=F32)
        linear(k, S["mT"], grp, D, lambda c0, c1: I["w_out"][l, :, c0:c1], [sg])
        P.barrier()


def stage_ln(k, l, own, which, xsrc, ysrc, xdst, dst_map=None, modulate=True):
    P, sb, PS, I, S = k.P, k.sb, k.PS, k.I, k.S
    sb.reset()
    ident = sb.f32(128)
    gp = sb.f32(D)
    lg = sb.f32(D)
    lb = sb.f32(D)
    P.dma("sp", ident, I["ident"], w=["ident"])
    o = which * 3 * D
    P.dma("sp", gp, S["mod"][l, o + 2 * D:o + 3 * D].partition_broadcast(128), w=["gp"])
    P.op("dve", lambda e: e.tensor_scalar(out=gp, in0=gp, scalar1=1.0, scalar2=None, op0=ALU.add), r=["gp"], w=["gp"])
    P.dma("sp", lg, I["ln1_g" if which == 0 else "ln2_g"][l, :].partition_broadcast(128), w=["lg"])
    P.dma("sp", lb, I["ln1_b" if which == 0 else "ln2_b"][l, :].partition_broadcast(128), w=["lb"])
    if modulate:
        scp = sb.f32(D)
        sh = sb.f32(D)
        P.dma("sp", sh, S["mod"][l, 3 * D:4 * D].partition_broadcast(128), w=["sh"])
        P.dma("sp", scp, S["mod"][l, 4 * D:5 * D].partition_broadcast(128), w=["scp"])
        P.op("dve", lambda e: e.tensor_scalar(out=scp, in0=scp, scalar1=1.0, scalar2=None, op0=ALU.add),
             r=["scp"], w=["scp"])
        ub = [sb.bf16(KC * 128).rearrange("p (a b) -> p a b", b=128) for _ in range(2)]
    xt = [sb.f32(D) for _ in range(2)]
    yt = [sb.f32(D) for _ in range(2)]
    st = [sb.f32(8 * 6) for _ in range(2)]
    mv = [sb.f32(4) for _ in range(2)]
    c = 0
    for tl in own:
        for q in range(4):
            tb = tl * 4 + q
            b = c % 2
            c += 1
            x, y = xt[b], yt[b]
            P.dma("sp", x, xsrc[tb * 128:(tb + 1) * 128, :], w=[("x", b)])
            P.dma("act", y, ysrc[tb * 128:(tb + 1) * 128, :], w=[("y", b)])
            P.op("pool", lambda e, y=y: e.tensor_tensor(out=y, in0=y, in1=gp, op=ALU.mult), r=[("y", b), "gp"],
                 w=[("y", b)])
            P.op("dve", lambda e, x=x, y=y: e.scalar_tensor_tensor(out=y, in0=x, scalar=float(ALPHA), in1=y,
                                                                   op0=ALU.mult, op1=ALU.add),
                 r=[("x", b), ("y", b)], w=[("y", b)])
            s6 = st[b]
            for j in range(8):
                P.op("dve", lambda e, s6=s6, y=y, j=j: e.bn_stats(out=s6[:, j * 6:(j + 1) * 6],
                                                                 in_=y[:, j * 512:(j + 1) * 512]),
                     r=[("y", b)], w=[("st", b)])
            m = mv[b]
            P.op("dve", lambda e, m=m, s6=s6: e.bn_aggr(out=m[:, 0:2], in_=s6), r=[("st", b)], w=[("mv", b)])
            P.op("dve", lambda e, m=m: e.tensor_scalar(out=m[:, 2:3], in0=m[:, 1:2], scalar1=float(LN_EPS),
                                                       scalar2=None, op0=ALU.add), r=[("mv", b)], w=[("mv", b)])
            P.op("act", lambda e, m=m: e.activation(out=m[:, 2:3], in_=m[:, 2:3], func=AF.Sqrt), r=[("mv", b)],
                 w=[("mv", b)])
            P.op("dve", lambda e, m=m: e.reciprocal(out=m[:, 3:4], in_=m[:, 2:3]), r=[("mv", b)], w=[("mv", b)])
            P.op("dve", lambda e, m=m, y=y: e.tensor_scalar(out=y, in0=y, scalar1=m[:, 0:1], scalar2=m[:, 3:4],
                                                            op0=ALU.subtract, op1=ALU.mult),
                 r=[("y", b), ("mv", b)], w=[("y", b)])
            P.op("pool", lambda e, y=y: e.tensor_tensor(out=y, in0=y, in1=lg, op=ALU.mult), r=[("y", b), "lg"],
                 w=[("y", b)])
            P.op("dve", lambda e, x=x, y=y: e.tensor_tensor(out=x, in0=y, in1=lb, op=ALU.add),
                 r=[("y", b), "lb", ("x", b)], w=[("x", b)])
            if dst_map is None:
                drows = xdst[tb * 128:(tb + 1) * 128, :]
            else:
                r0 = dst_map[tl] * 512 + q * 128
                drows = xdst[r0:r0 + 128, :]
            P.dma("sp", drows, x, r=[("x", b)], w=[], is_out=(dst_map is not None))
            if modulate:
                P.op("pool", lambda e, x=x, y=y: e.tensor_tensor(out=y, in0=x, in1=scp, op=ALU.mult),
                     r=[("x", b), "scp", ("y", b)], w=[("y", b)])
                P.op("dve", lambda e, y=y: e.tensor_tensor(out=y, in0=y, in1=sh, op=ALU.add),
                     r=[("y", b), "sh"], w=[("y", b)])
                for g in range(KC // 4):
                    pb = g % 2
                    for j in range(4):
                        fc = g * 4 + j
                        P.op("pe", lambda e, y=y, fc=fc, pb=pb, j=j: e.transpose(
                            out=PS[pb][:, j * 128:(j + 1) * 128], in_=y[:, fc * 128:(fc + 1) * 128],
                            identity=ident), r=[("y", b), "ident"], w=[("pst", pb)])
                    dst = ub[b][:, g * 4:(g + 1) * 4, :]
                    src = PS[pb][:, :].rearrange("p (a b) -> p a b", b=128)
                    if g % 2 == 0:
                        P.op("act", lambda e, dst=dst, src=src: e.copy(out=dst, in_=src), r=[("pst", pb)],
                             w=[("ub", b)])
                    else:
                        P.op("dve", lambda e, dst=dst, src=src: e.tensor_copy(out=dst, in_=src), r=[("pst", pb)],
                             w=[("ub", b)])
                P.dma("sp", S["uT"][:, tb * 128:(tb + 1) * 128].rearrange("(kc p) t -> p kc t", p=128), ub[b],
                      r=[("ub", b)], w=[])


PEER_MARGIN = 2e-4


def stage_wq(k, l, own):
    P, sb, I, S = k.P, k.sb, k.I, k.S
    for grp in own_groups(own):
        sb.reset()
        sg = dict(name="qT", c0=0, c1=2048, mode="F", out=S["qT"], tiles=list(grp), odt=F32)
        linear(k, S["uT"], grp, D, lambda c0, c1: I["peer_wq"][l, :, c0:c1], [sg])
        P.barrier()


def stage_peer_sel(k, l, own):
    P, sb, PS, I, S = k.P, k.sb, k.PS, k.I, k.S
    sb.reset()
    ident = sb.f32(128)
    skT = sb.f32(16 * 128).rearrange("p (a b) -> p a b", b=128)
    P.dma("sp", ident, I["ident"], w=["ident"])
    P.dma("sp", skT, I["peer_skT"][l].rearrange("a d k -> d a k"), w=["skT"])
    qt = [sb.f32(16 * 128).rearrange("p (a b) -> p a b", b=128) for _ in range(2)]
    s12 = [sb.f32(256) for _ in range(2)]
    tmp = [sb.f32(256) for _ in range(2)]
    m12 = [sb.f32(32) for _ in range(2)]
    cand = [sb.f32(256) for _ in range(2)]
    c16 = [sb.f32(16) for _ in range(2)]
    sm = [sb.f32(8) for _ in range(2)]
    o3 = [sb.f32(384) for _ in range(2)]
    o3t = [sb.f32(384) for _ in range(2)]
    hlb = [sb.bf16(512).rearrange("p (a b) -> p a b", b=128) for _ in range(2)]
    c = 0
    for tl in own:
        for q in range(4):
            tb = tl * 4 + q
            qb = tb % 2
            P.dma("sp", qt[qb], S["qT"][:, tb * 128:(tb + 1) * 128].rearrange("(c p) t -> p c t", p=128),
                  w=[("qt", qb)])
            for h in range(8):
                b = c % 2
                c += 1
                ps = PS[b]
                for p2 in range(2):
                    P.op("pe", lambda e, ps=ps, p2=p2, h=h, qb=qb: e.matmul(
                        ps[:, p2 * 128:(p2 + 1) * 128], qt[qb][:, 2 * h + p2, :], skT[:, 2 * h + p2, :],
                        start=(p2 == 0), stop=True, skip_group_check=True),
                        r=[("qt", qb), "skT"], w=[("pss", b)])
                s = s12[b]
                P.op("act", lambda e, s=s, ps=ps: e.copy(out=s, in_=ps[:, 0:256]), r=[("pss", b)], w=[("s12", b)])
                t = tmp[b]
                m = m12[b]
                for p2 in range(2):
                    sv = s[:, p2 * 128:(p2 + 1) * 128]
                    tv = t[:, p2 * 128:(p2 + 1) * 128]
                    P.op("dve", lambda e, m=m, sv=sv, p2=p2: e.max(out=m[:, p2 * 16:p2 * 16 + 8], in_=sv),
                         r=[("s12", b)], w=[("m12", b)])
                    P.op("dve", lambda e, m=m, sv=sv, tv=tv, p2=p2: e.match_replace(
                        out=tv, in_to_replace=m[:, p2 * 16:p2 * 16 + 8], in_values=sv, imm_value=-3.0e38),
                        r=[("s12", b), ("m12", b)], w=[("tmp", b)])
                    P.op("dve", lambda e, m=m, tv=tv, p2=p2: e.max(out=m[:, p2 * 16 + 8:p2 * 16 + 16], in_=tv),
                         r=[("tmp", b)], w=[("m12", b)])
                cd = cand[b]
                for a in range(16):
                    P.op("dve", lambda e, cd=cd, m=m, a=a: e.tensor_scalar(
                        out=cd[:, a * 16:(a + 1) * 16], in0=m[:, 16:32], scalar1=m[:, a:a + 1], scalar2=None,
                        op0=ALU.add), r=[("m12", b)], w=[("cand", b)])
                cc = c16[b]
                P.op("dve", lambda e, cc=cc, cd=cd: e.max(out=cc[:, 0:8], in_=cd), r=[("cand", b)], w=[("c16", b)])
                P.op("dve", lambda e, cc=cc, cd=cd, t=t: e.match_replace(out=t, in_to_replace=cc[:, 0:8],
                                                                         in_values=cd, imm_value=-3.0e38),
                     r=[("cand", b), ("c16", b)], w=[("tmp", b)])
                P.op("dve", lambda e, cc=cc, t=t: e.max(out=cc[:, 8:16], in_=t), r=[("tmp", b)], w=[("c16", b)])
                z = sm[b]
                P.op("dve", lambda e, z=z, cc=cc: e.tensor_scalar(out=z[:, 0:1], in0=cc[:, 0:1], scalar1=-1.0,
                                                                  scalar2=None, op0=ALU.mult),
                     r=[("c16", b)], w=[("sm", b)])
                P.op("act", lambda e, z=z, cc=cc, t=t: e.activation(out=t[:, 0:16], in_=cc, func=AF.Exp,
                                                                    bias=z[:, 0:1], accum_out=z[:, 1:2]),
                     r=[("c16", b), ("sm", b), ("tmp", b)], w=[("sm", b), ("tmp", b)])
                P.op("act", lambda e, z=z: e.activation(out=z[:, 2:3], in_=z[:, 1:2], func=AF.Ln),
                     r=[("sm", b)], w=[("sm", b)])
                P.op("dve", lambda e, z=z, cc=cc: e.tensor_tensor(out=z[:, 2:3], in0=z[:, 2:3], in1=cc[:, 0:1],
                                                                  op=ALU.add), r=[("sm", b), ("c16", b)],
                     w=[("sm", b)])
                P.op("dve", lambda e, z=z, cc=cc: e.tensor_scalar(out=z[:, 3:4], in0=cc[:, 15:16],
                                                                  scalar1=-PEER_MARGIN, scalar2=None, op0=ALU.add),
                     r=[("c16", b)], w=[("sm", b)])
                o = o3[b]
                P.op("dve", lambda e, o=o, s=s, z=z: e.tensor_scalar(out=o[:, 0:128], in0=s[:, 0:128],
                                                                     scalar1=z[:, 3:4], scalar2=None,
                                                                     op0=ALU.subtract),
                     r=[("s12", b), ("sm", b)], w=[("o3", b)])
                P.op("dve", lambda e, o=o, s=s, z=z: e.tensor_scalar(out=o[:, 128:256], in0=s[:, 0:128],
                                                                     scalar1=z[:, 2:3], scalar2=None,
                                                                     op0=ALU.subtract),
                     r=[("s12", b), ("sm", b)], w=[("o3", b)])
                P.op("pool", lambda e, o=o, s=s: e.tensor_copy(out=o[:, 256:384], in_=s[:, 128:256]),
                     r=[("s12", b)], w=[("o3", b)])
                pt = PS[2 + b]
                for j in range(3):
                    P.op("pe", lambda e, pt=pt, o=o, j=j: e.transpose(out=pt[:, j * 128:(j + 1) * 128],
                                                                     in_=o[:, j * 128:(j + 1) * 128],
                                                                     identity=ident),
                         r=[("o3", b), "ident"], w=[("pst", b)])
                ot = o3t[b]
                P.op("act", lambda e, ot=ot, pt=pt: e.copy(out=ot, in_=pt[:, 0:384]), r=[("pst", b)],
                     w=[("o3t", b)])
                hl = hlb[b]
                for j, eng in ((0, "dve"), (1, "pool")):
                    src = ot[:, j * 128:(j + 1) * 128]
                    P.op(eng, lambda e, hl=hl, src=src, j=j: e.tensor_copy(out=hl[:, 2 * j, :], in_=src),
                         r=[("o3t", b)], w=[("hl", b, j)])
                    P.op(eng, lambda e, hl=hl, src=src, j=j: e.tensor_tensor(out=hl[:, 2 * j + 1, :], in0=src,
                                                                             in1=hl[:, 2 * j, :], op=ALU.subtract),
                         r=[("o3t", b), ("hl", b, j)], w=[("hl", b, j)])
                    for q2 in range(2):
                        P.dma("act", S["S1H"][2 * j + q2, h, :, tb * 128:(tb + 1) * 128], hl[:, 2 * j + q2, :],
                              r=[("hl", b, j)], w=[])
                P.dma("sp", S["S2T"][h, :, tb * 128:(tb + 1) * 128], ot[:, 256:384], r=[("o3t", b)], w=[])


def stage_peer_dense(k, l, own):
    P, sb, PS, I, S = k.P, k.sb, k.PS, k.I, k.S
    for grp in own_groups(own):
        ng = len(grp)
        NTK = 512 * ng
        sb.reset()
        ident = sb.f32(128)
        OH = sb.bf16(8 * 128, parts=16).rearrange("p (a b) -> p a b", b=128)
        P.dma("sp", ident, I["ident"], w=["ident"])
        P.dma("sp", OH, I["oh16"], w=["OH"])
        XT = sb.bf16(KC * NTK).rearrange("p (a b) -> p a b", b=NTK)
        s2T = sb.f32(8 * NTK).rearrange("p (a b) -> p a b", b=NTK)
        for gi, t in enumerate(grp):
            ts = slice(t * 512, (t + 1) * 512)
            P.dma("sp", XT[:, :, gi * 512:(gi + 1) * 512], S["uT"][:, ts].rearrange("(kc p) t -> p kc t", p=128),
                  w=["XT"])
            P.dma("act", s2T[:, :, gi * 512:(gi + 1) * 512], S["S2T"][:, :, ts].rearrange("h j t -> j h t"),
                  w=["s2T"])
        Ub = [sb.f32(D) for _ in range(2)]
        RA = [sb.bf16(NTK, parts=16) for _ in range(2)]
        RL = [sb.bf16(NTK, parts=16) for _ in range(2)]
        UT = sb.bf16(KC * 128).rearrange("p (a b) -> p a b", b=128)
        ga32 = sb.f32(NTK)
        zA = [sb.f32(NTK) for _ in range(2)]
        zB = [sb.f32(NTK) for _ in range(2)]
        Wacc = sb.f32(NTK)
        GAb = [sb.bf16(NTK) for _ in range(2)]
        nh = NTK // 512
        zc = 0
        for i in range(128):
            b = i % 2
            P.dma("sp", Ub[b], I["peer_u"][l, i * 128:(i + 1) * 128, :], w=[("Ub", b)])
            for gi, t in enumerate(grp):
                ts = slice(t * 512, (t + 1) * 512)
                for q2 in range(2):
                    P.dma("act", RA[b][q2 * 8:(q2 + 1) * 8, gi * 512:(gi + 1) * 512], S["S1H"][q2, :, i, ts],
                          w=[("RA", b)])
                    P.dma("act", RL[b][q2 * 8:(q2 + 1) * 8, gi * 512:(gi + 1) * 512], S["S1H"][2 + q2, :, i, ts],
                          w=[("RL", b)])
            for g in range(KC // 4):
                pb = 6 + g % 2
                for j in range(4):
                    kc = g * 4 + j
                    P.op("pe", lambda e, pb=pb, j=j, kc=kc, b=b: e.transpose(
                        out=PS[pb][:, j * 128:(j + 1) * 128], in_=Ub[b][:, kc * 128:(kc + 1) * 128],
                        identity=ident), r=[("Ub", b), "ident"], w=[("pst", pb)])
                dst = UT[:, g * 4:(g + 1) * 4, :]
                src = PS[pb][:, :].rearrange("p (a b) -> p a b", b=128)
                if g % 2 == 0:
                    P.op("act", lambda e, dst=dst, src=src: e.copy(out=dst, in_=src), r=[("pst", pb)], w=["UT"])
                else:
                    P.op("dve", lambda e, dst=dst, src=src: e.tensor_copy(out=dst, in_=src), r=[("pst", pb)],
                         w=["UT"])
            for hf in range(nh):
                for kc in range(KC):
                    P.op("pe", lambda e, hf=hf, kc=kc: e.matmul(PS[hf], UT[:, kc, :],
                                                                XT[:, kc, hf * 512:(hf + 1) * 512],
                                                                start=(kc == 0), stop=(kc == KC - 1)),
                         r=["UT", "XT"], w=[("psA", hf)])
                P.op("act", lambda e, hf=hf: e.activation(out=ga32[:, hf * 512:(hf + 1) * 512], in_=PS[hf],
                                                          func=GELU), r=[("psA", hf)], w=["ga32"])
            for h in range(8):
                zb = zc % 2
                zc += 1
                for hf in range(nh):
                    P.op("pe", lambda e, h=h, hf=hf, b=b: e.matmul(PS[2 + hf], OH[:, h, :],
                                                                   RA[b][:, hf * 512:(hf + 1) * 512],
                                                                   start=True, stop=True),
                         r=["OH", ("RA", b)], w=[("psbA", hf)])
                    P.op("pe", lambda e, h=h, hf=hf, b=b: e.matmul(PS[4 + hf], OH[:, h, :],
                                                                   RL[b][:, hf * 512:(hf + 1) * 512],
                                                                   start=True, stop=True),
                         r=["OH", ("RL", b)], w=[("psbL", hf)])
                    P.op("dve", lambda e, h=h, hf=hf, zb=zb: e.tensor_tensor(
                        out=zA[zb][:, hf * 512:(hf + 1) * 512], in0=PS[2 + hf],
                        in1=s2T[:, h, hf * 512:(hf + 1) * 512], op=ALU.add),
                        r=[("psbA", hf), "s2T"], w=[("zA", zb)])
                    P.op("act", lambda e, hf=hf, zb=zb: e.copy(out=zB[zb][:, hf * 512:(hf + 1) * 512],
                                                               in_=PS[4 + hf]),
                         r=[("psbL", hf)], w=[("zB", zb)])
                P.op("pool", lambda e, h=h, zb=zb: e.tensor_tensor(out=zB[zb], in0=zB[zb], in1=s2T[:, h, :],
                                                                   op=ALU.add), r=[("zB", zb), "s2T"],
                     w=[("zB", zb)])
                P.op("act", lambda e, zb=zb: e.activation(out=zB[zb], in_=zB[zb], func=AF.Exp), r=[("zB", zb)],
                     w=[("zB", zb)])
                if h == 0:
                    P.op("dve", lambda e, zb=zb: e.scalar_tensor_tensor(out=Wacc, in0=zA[zb], scalar=0.0,
                                                                        in1=zB[zb], op0=ALU.is_ge, op1=ALU.mult),
                         r=[("zA", zb), ("zB", zb)], w=["Wacc"])
                else:
                    P.op("dve", lambda e, zb=zb: e.scalar_tensor_tensor(out=zA[zb], in0=zA[zb], scalar=0.0,
                                                                        in1=zB[zb], op0=ALU.is_ge, op1=ALU.mult),
                         r=[("zA", zb), ("zB", zb)], w=[("zA", zb)])
                    P.op("pool" if h % 2 == 1 else "dve",
                         lambda e, zb=zb: e.tensor_tensor(out=Wacc, in0=Wacc, in1=zA[zb], op=ALU.add),
                         r=[("zA", zb), "Wacc"], w=["Wacc"])
            P.op("dve", lambda e, b=b: e.tensor_tensor(out=GAb[b], in0=ga32, in1=Wacc, op=ALU.mult),
                 r=["ga32", "Wacc"], w=[("GAb", b)])
            for gi, t in enumerate(grp):
                P.dma("sp", S["GA"][i, :, t * 512:(t + 1) * 512], GAb[b][:, gi * 512:(gi + 1) * 512],
                      r=[("GAb", b)], w=[])
        P.barrier()
        sb.reset()
        ntb = NTK // 128
        EC = 8
        HC = 2048
        accsb = sb.f32(ntb * HC).rearrange("p (a b) -> p a b", b=HC)
        Vc = [sb.bf16(EC * HC).rearrange("p (a b) -> p a b", b=HC) for _ in range(2)]
        GAc = [sb.bf16(EC * NTK).rearrange("p (a b) -> p a b", b=NTK) for _ in range(2)]
        cc = 0
        pc = 0
        for half in range(D // HC):
            for ch in range(128 // EC):
                b = cc % 2
                cc += 1
                for i in range(EC):
                    e0 = (ch * EC + i) * 128
                    P.dma("pool", Vc[b][:, i, :], I["peer_v"][l, e0:e0 + 128, half * HC:(half + 1) * HC],
                          w=[("Vc", b)])
                for gi, t in enumerate(grp):
                    P.dma("sp", GAc[b][:, :, gi * 512:(gi + 1) * 512],
                          S["GA"][ch * EC:(ch + 1) * EC, :, t * 512:(t + 1) * 512].rearrange("i e t -> e i t"),
                          w=[("GAc", b)])
                for tb in range(ntb):
                    for cq in range(HC // 512):
                        pi = pc % 8
                        pc += 1
                        for i in range(EC):
                            P.op("pe", lambda e, pi=pi, b=b, i=i, tb=tb, cq=cq: e.matmul(
                                PS[pi], GAc[b][:, i, tb * 128:(tb + 1) * 128], Vc[b][:, i, cq * 512:(cq + 1) * 512],
                                start=(i == 0), stop=(i == EC - 1)),
                                r=[("GAc", b), ("Vc", b)], w=[("psY", pi)])
                        dst = accsb[:, tb, cq * 512:(cq + 1) * 512]
                        if ch == 0:
                            P.op("act", lambda e, dst=dst, pi=pi: e.copy(out=dst, in_=PS[pi]), r=[("psY", pi)],
                                 w=[("accsb", tb, cq)])
                        else:
                            P.op("dve", lambda e, dst=dst, pi=pi: e.tensor_tensor(out=dst, in0=dst, in1=PS[pi],
                                                                                  op=ALU.add),
                                 r=[("psY", pi), ("accsb", tb, cq)], w=[("accsb", tb, cq)])
            for tb in range(ntb):
                t = grp[tb // 4]
                r0 = t * 512 + (tb % 4) * 128
                P.dma("sp", S["y2"][r0:r0 + 128, half * HC:(half + 1) * HC], accsb[:, tb, :],
                      r=[("accsb", tb, cq) for cq in range(HC // 512)], w=[("y2out", tb)])
        P.barrier()


_NC_CACHE = {}


def _get_nc():
    if "k" not in _NC_CACHE:
        _NC_CACHE["k"] = build(layers=(0, 1), final_out=True)
    return _NC_CACHE["k"]


def kernel(x, c, w_ada, b_ada, w_in, b_forget, cmp_pe, cmp_w1, cmp_w2, w_branch, w_gate, b_gate, w_out,
           ln1_g, ln1_b, peer_wq, peer_subkeys, peer_u, peer_v, ln2_g, ln2_b):
    from concourse.bass_utils import run_bass_kernel_spmd
    f = lambda a: np.ascontiguousarray(np.asarray(a, dtype=np.float32))
    x = f(x)
    c = f(c)
    B = x.shape[0]
    shared = dict(w_ada=f(w_ada), b_ada=f(b_ada), w_in=f(w_in), b_forget=f(b_forget), cmp_w1=f(cmp_w1),
                  cmp_w2=f(cmp_w2), w_branch=f(w_branch), w_gate=f(w_gate), w_out=f(w_out), ln1_g=f(ln1_g),
                  ln1_b=f(ln1_b), ln2_g=f(ln2_g), ln2_b=f(ln2_b), peer_wq=f(peer_wq), peer_u=f(peer_u),
                  peer_v=f(peer_v))
    shared["cmp_peT"] = np.ascontiguousarray(f(cmp_pe).transpose(0, 1, 3, 2))
    shared["b_gateT"] = np.ascontiguousarray(f(b_gate).reshape(L, 96, 128).transpose(0, 2, 1))
    shared["peer_skT"] = np.ascontiguousarray(f(peer_subkeys).reshape(L, 16, 128, 128).transpose(0, 1, 3, 2))
    import ml_dtypes
    oh = np.zeros((16, 8, 128), np.float32)
    for i in range(16):
        oh[i, i % 8, :] = 1.0
    shared["oh16"] = oh.astype(ml_dtypes.bfloat16)
    tables = [host_tables(0), host_tables(1)]
    k = _get_nc()
    names = [a.memorylocations[0].name for a in k.nc.m.functions[0].allocations
             if isinstance(a, mybir.MemoryLocationSet) and a.kind == "ExternalInput"]
    maps = []
    for b in range(B):
        for h in (0, 1):
            xc = np.zeros((T, D), np.float32)
            if h == 0:
                xc[512:] = x[b, :1536]
            else:
                xc[:] = x[b]
            m = dict(shared)
            m.update(tables[h])
            m["xctx"] = xc
            m["cT"] = np.ascontiguousarray(c[b].reshape(KC, 128).T)
            maps.append({nm: m[nm] for nm in names if nm in m})
    res = run_bass_kernel_spmd(k.nc, maps, core_ids=list(range(2 * B)))
    out = np.zeros((B, 2048, D), np.float32)
    for b in range(B):
        for h in (0, 1):
            o = np.asarray(res.results[b * 2 + h]["out"], dtype=np.float32)
            out[b, 512 * h:512 * h + 512] = o[0:512]
            out[b, 1024 + 512 * h:1024 + 512 * h + 512] = o[512:1024]
    return out
```

```python
import numpy as np
import concourse.bass as bass
import concourse.mybir as mybir

F32 = mybir.dt.float32
BF16 = mybir.dt.bfloat16
AF = mybir.ActivationFunctionType
ALU = mybir.AluOpType
AX = mybir.AxisListType


class Prog:
    def __init__(self, nc, n_dma_sp=24, n_dma_pool=12, n_dma_act=8):
        self.nc = nc
        self.ops = []
        self.lastw = {}
        self.readers = {}
        self.eng_last = {}
        self.dma_since_barrier = []
        self.ndma = {"sp": n_dma_sp, "pool": n_dma_pool, "act": n_dma_act}
        self.out_dmas = []

    def op(self, eng, fn, r=(), w=(), dma=False, extra=()):
        deps = set(extra)
        for k in r:
            if k in self.lastw:
                deps.add(self.lastw[k])
        for k in w:
            if k in self.lastw:
                deps.add(self.lastw[k])
            deps.update(self.readers.get(k, ()))
        idx = len(self.ops)
        self.ops.append(dict(eng=eng, fn=fn, deps=deps, dma=dma))
        for k in r:
            self.readers.setdefault(k, []).append(idx)
        for k in w:
            self.lastw[k] = idx
            self.readers[k] = []
        if dma:
            self.dma_since_barrier.append(idx)
        else:
            self.eng_last[eng] = idx
        return idx

    def dma(self, q, out, in_, r=(), w=(), is_out=False, **kw):
        def fn(e, out=out, in_=in_, kw=kw):
            return e.dma_start(out=out, in_=in_, **kw)
        i = self.op(q, fn, r=r, w=w, dma=True)
        if is_out:
            self.out_dmas.append(i)
        return i

    def barrier(self):
        deps = set(self.eng_last.values()) | set(self.dma_since_barrier)
        self.dma_since_barrier = []
        for eng in ("pe", "act", "dve", "pool", "sp"):
            idx = len(self.ops)
            self.ops.append(dict(eng=eng, fn=None, deps=set(deps), dma=False, barrier=True))
            self.eng_last[eng] = idx
        self.lastw = {}
        self.readers = {}

    def emit(self, stack):
        nc = self.nc
        engs = {"pe": nc.tensor, "act": nc.scalar, "dve": nc.vector, "pool": nc.gpsimd, "sp": nc.sync}
        csem = {e: stack.enter_context(nc.semaphore("c_" + e)) for e in ("pe", "act", "dve", "pool")}
        dsems = {q: [stack.enter_context(nc.semaphore(f"d_{q}{i}")) for i in range(n)]
                 for q, n in self.ndma.items()}
        ops = self.ops
        hasdep = [False] * len(ops)
        for o in ops:
            for d in o["deps"]:
                if ops[d]["eng"] == "pe" and o["eng"] == "pe" and not o["dma"] and not ops[d]["dma"] \
                        and not o.get("barrier"):
                    continue
                hasdep[d] = True
        for i in self.out_dmas:
            hasdep[i] = True
        ccount = {e: 0 for e in csem}
        dnext = {q: 0 for q in dsems}
        duse = {q: [0] * len(dsems[q]) for q in dsems}
        comp = [None] * len(ops)
        pre_wait = [None] * len(ops)
        for i, o in enumerate(ops):
            e = o["eng"]
            if o.get("barrier"):
                continue
            if o["dma"]:
                s = dnext[e]
                dnext[e] = (s + 1) % len(dsems[e])
                if duse[e][s] > 0:
                    pre_wait[i] = (dsems[e][s], 16 * duse[e][s])
                duse[e][s] += 1
                comp[i] = (dsems[e][s], 16 * duse[e][s], None)
            else:
                if hasdep[i]:
                    ccount[e] += 1
                    comp[i] = (csem[e], ccount[e], e)
        per_eng = {e: [] for e in engs}
        for i, o in enumerate(ops):
            per_eng[o["eng"]].append(i)
        block = stack.enter_context(nc.Block())
        final_waits = [comp[i][:2] for i in self.out_dmas]
        self.n_instr = 0

        def make(ename):
            def body(eng):
                known = {}

                def wait(sem, val):
                    k = id(sem)
                    if known.get(k, 0) >= val:
                        return
                    known[k] = val
                    eng.wait_ge(sem, val)
                    self.n_instr += 1
                for i in per_eng[ename]:
                    o = ops[i]
                    for d in sorted(o["deps"]):
                        c = comp[d]
                        if c is None:
                            continue
                        if c[2] == "pe" and ename == "pe" and not o["dma"]:
                            continue
                        wait(c[0], c[1])
                    if o.get("barrier"):
                        continue
                    if pre_wait[i] is not None:
                        wait(*pre_wait[i])
                    ins = o["fn"](eng)
                    self.n_instr += 1
                    if o["dma"]:
                        ins.then_inc(comp[i][0], 16)
                    elif comp[i] is not None:
                        ins.then_inc(comp[i][0], 1)
                if ename == "sp":
                    for s, v in final_waits:
                        wait(s, v)
            return body
        block.tensor(make("pe"))
        block.scalar(make("act"))
        block.vector(make("dve"))
        block.gpsimd(make("pool"))
        block.sync(make("sp"))


class Arena:
    def __init__(self, ap, ncols):
        self.ap = ap
        self.ncols = ncols
        self.off = 0

    def reset(self):
        self.off = 0

    def alloc(self, cols, parts=128):
        cols_al = (cols + 7) // 8 * 8
        assert self.off + cols_al <= self.ncols, (self.off, cols, self.ncols)
        v = self.ap[0:parts, self.off:self.off + cols]
        self.off += cols_al
        return v


from contextlib import ExitStack

D = 4096
T = 2048
NT = 4
HD = 128
IN_COLS = 11564
L = 2
KC = D // 128

SEG = dict(qa=(0, 1536), kcmp=(1536, 1920), vcmp=(1920, 2304), kslc=(2304, 2688), vslc=(2688, 3072),
           kwin=(3072, 3456), vwin=(3456, 3840), gates=(3840, 3876), foxq=(3876, 4900), foxk=(4900, 5924),
           foxv=(5924, 6948), forget=(6948, 6956), dilq=(6956, 8492), dilk=(8492, 10028), dilv=(10028, 11564))
F_SEGS = ["qa", "kcmp", "vcmp", "kslc", "kwin", "foxq", "foxk", "dilq", "dilk", "forget"]
T_SEGS = ["vslc", "vwin", "gates", "foxv", "dilv"]
Q_SEGS = ["qa", "gates", "foxq", "dilq"]


class SB:
    def __init__(self, arena):
        self.a = arena
        self.off = 0
        self.n = arena.shape[1]

    def reset(self):
        self.off = 0

    def f32(self, cols, parts=128):
        c = (cols + 1) // 2 * 2
        assert self.off + c <= self.n, ("SBUF arena overflow", self.off, c)
        v = self.a[0:parts, self.off:self.off + cols]
        self.off += c
        return v

    def bf16(self, cols, parts=128):
        c = (cols + 3) // 4 * 2
        assert self.off + c <= self.n, ("SBUF arena overflow", self.off, c)
        v = self.a[0:parts, self.off:self.off + c].bitcast(BF16)[:, 0:cols]
        self.off += c
        return v


EXTRA_INPUTS = {}


class K:
    pass


def build(stages=None, debug=(), inject=(), layers=(0,), own_by_layer=None, final_out=False):
    nc = bass.Bass("TRN2", target_bir_lowering=False)
    k = K()
    k.nc = nc
    k.debug = set(debug)
    k.inject = set(inject)
    k.dbg_out = {}

    def din(name, shape, dt=F32):
        return nc.dram_tensor(name, list(shape), dt, kind="ExternalInput").ap()

    def scr(name, shape, dt=F32):
        kind = "ExternalOutput" if name in k.debug else ("ExternalInput" if name in k.inject else "Internal")
        return nc.dram_tensor(name, list(shape), dt, kind=kind).ap()
    shapes = dict(xctx=([T, D], F32), cT=([128, KC], F32), ident=([128, 128], F32), tabs=([128, TABW], F32),
                  kbias=([128, 16], F32), dcmp=([128, T], F32), cm=([128, 32], F32), Emat=([32, T], BF16),
                  smul=([T, 32], F32), sadd=([T, 32], F32), tri=([128, 5, 512], F32),
                  w_ada=([L, D, 6 * D], F32), b_ada=([L, 6 * D], F32), w_in=([L, D, IN_COLS], F32),
                  b_forget=([L, 8], F32), cmp_peT=([L, 2, 128, 32], F32), cmp_w1=([L, 2, 4096, 128], F32),
                  cmp_w2=([L, 2, 128, 128], F32), b_gateT=([L, 128, 96], F32), w_gate=([L, D, 3 * D], F32),
                  w_branch=([L, 3072, D], F32), w_out=([L, D, D], F32), ln1_g=([L, D], F32), ln1_b=([L, D], F32),
                  ln2_g=([L, D], F32), ln2_b=([L, D], F32), peer_wq=([L, D, 2048], F32),
                  peer_skT=([L, 16, 128, 128], F32), peer_u=([L, 16384, D], F32), peer_v=([L, 16384, D], F32),
                  oh16=([16, 8, 128], BF16))
    shapes.update(EXTRA_INPUTS)

    class Lazy(dict):
        def __missing__(self, name):
            sh, dt = shapes[name]
            self[name] = din(name, sh, dt)
            return self[name]
    I = Lazy()
    k.I = I
    S = {}
    S["mod"] = scr("mod", [L, 6 * D])
    S["uT"] = scr("uT", [D, T], BF16)
    for s in F_SEGS:
        S[s] = scr("p_" + s, [SEG[s][1] - SEG[s][0], T], F32 if s == "forget" else BF16)
    for s in T_SEGS:
        S[s] = scr("p_" + s, [T, SEG[s][1] - SEG[s][0]], F32 if s in ("gates", "forget") else BF16)
    S["kc"] = scr("kc", [3, 128, 128], BF16)
    S["vc"] = scr("vc", [3, 128, 128], F32)
    S["Frow"] = scr("Frow", [8, T])
    S["nFrow"] = scr("nFrow", [8, T])
    S["oT"] = scr("oT", [3072, T], BF16)
    S["gT"] = scr("gT", [3 * D, T], BF16)
    for nm in ("ya", "yb", "yc", "mT"):
        S[nm] = scr(nm, [D, T], BF16)
    for nm in ("y", "x1", "y2", "x2"):
        S[nm] = scr(nm, [T, D], F32)
    S["qT"] = scr("qT", [2048, T], F32)
    S["S2T"] = scr("S2T", [8, 128, T], F32)
    S["S1H"] = scr("S1H", [4, 8, 128, T], BF16)
    S["GA"] = scr("GA", [128, 128, T], BF16)
    k.out = nc.dram_tensor("out", [1024, D], F32, kind="ExternalOutput").ap()
    k.S = S
    if stages is None:
        stages = ["mod", "modulate", "win", "cmp", "cumsum", "nsa", "fox", "dil", "gates", "branch", "merge", "out",
                  "ln1", "wq", "psel", "pdense", "ln2"]
    if own_by_layer is None:
        own_by_layer = {0: (0, 1, 2, 3), 1: (1, 3)}

    stack = ExitStack()
    with stack:
        arena = stack.enter_context(nc.sbuf_tensor("arena", [128, 49152], F32))
        k.sb = SB(arena[:])
        ps01 = [stack.enter_context(nc.psum_tensor(f"ps{i}", [128, 512], F32))[:] for i in range(2)]
        accs = [stack.enter_context(nc.psum_tensor(f"acc{i}", [128, 2, 512], F32))[:] for i in range(2)]
        ps67 = [stack.enter_context(nc.psum_tensor(f"ps{i}", [128, 512], F32))[:] for i in (6, 7)]
        k.ACC = accs
        k.PS = ps01 + [accs[0][:, 0, :], accs[0][:, 1, :], accs[1][:, 0, :], accs[1][:, 1, :]] + ps67
        P = Prog(nc)
        k.P = P
        if "mod" in stages:
            stage_mod(k)
            P.barrier()
        for li, l in enumerate(layers):
            own = own_by_layer[l]
            xsrc = I["xctx"] if (li == 0 and "x2" not in k.inject) else S["x2"]
            last = (li == len(layers) - 1)
            if "modulate" in stages:
                stage_modulate(k, l, xsrc, 0)
                P.barrier()
            if "win" in stages:
                stage_win(k, l, own_tiles=own)
            for nm, fn in (("cmp", stage_cmp), ("cumsum", stage_cumsum)):
                if nm in stages:
                    fn(k, l)
                    P.barrier()
            for nm, fn in (("nsa", stage_nsa), ("fox", stage_fox), ("dil", stage_dil), ("gates", stage_gates),
                           ("branch", stage_branch), ("merge", stage_merge), ("out", stage_out)):
                if nm in stages:
                    fn(k, l, own)
                    P.barrier()
            if "ln1" in stages:
                stage_ln(k, l, own, 0, xsrc, S["y"], S["x1"], modulate=True)
                P.barrier()
            for nm, fn in (("wq", stage_wq), ("psel", stage_peer_sel), ("pdense", stage_peer_dense)):
                if nm in stages:
                    fn(k, l, own)
                    P.barrier()
            if "ln2" in stages:
                if last and final_out:
                    stage_ln(k, l, own, 1, S["x1"], S["y2"], k.out, dst_map={t: i for i, t in enumerate(own)},
                             modulate=False)
                else:
                    stage_ln(k, l, own, 1, S["x1"], S["y2"], S["x2"], modulate=False)
                P.barrier()
        P.emit(stack)
    k.n_instr = P.n_instr
    return k


def stage_mod(k):
    P, sb, PS, I, S = k.P, k.sb, k.PS, k.I, k.S
    sb.reset()
    cT = sb.f32(KC)
    NB = 256
    mrow = [sb.f32(NB, parts=1) for _ in range(2)]
    brow = [sb.f32(NB, parts=1) for _ in range(2)]
    wb = [sb.f32(KC * NB).rearrange("p (a b) -> p a b", b=NB) for _ in range(2)]
    P.dma("sp", cT, I["cT"][:, :], w=["cT"])
    for l in range(L):
        for cb in range(6 * D // NB):
            b = cb % 2
            P.dma("sp", brow[b], I["b_ada"][l:l + 1, cb * NB:(cb + 1) * NB], w=[("brow", b)])
            P.dma("sp", wb[b], I["w_ada"][l, :, cb * NB:(cb + 1) * NB].rearrange("(kc p) n -> p kc n", p=128),
                  w=[("wada", b)])
            ps = PS[b][0:1, 0:NB]
            for kc in range(KC):
                P.op("pe", lambda e, ps=ps, kc=kc, b=b: e.matmul(ps, cT[:, kc:kc + 1], wb[b][:, kc, :],
                                                               start=(kc == 0), stop=(kc == KC - 1)),
                     r=["cT", ("wada", b)], w=[("psm", b)])
            P.op("dve", lambda e, ps=ps, b=b: e.tensor_tensor(out=mrow[b], in0=ps, in1=brow[b], op=ALU.add),
                 r=[("psm", b), ("brow", b)], w=[("mrow", b)])
            P.dma("sp", S["mod"][l:l + 1, cb * NB:(cb + 1) * NB], mrow[b], r=[("mrow", b)], w=[])


def stage_modulate(k, l, xsrc, which):
    P, sb, PS, I, S = k.P, k.sb, k.PS, k.I, k.S
    sb.reset()
    ident = sb.f32(128)
    scp = sb.f32(D)
    sh = sb.f32(D)
    xt = [sb.f32(D) for _ in range(2)]
    ub = [sb.bf16(KC * 128).rearrange("p (a b) -> p a b", b=128) for _ in range(2)]
    o = which * 3 * D
    P.dma("sp", ident, I["ident"][:, :], w=["ident"])
    P.dma("sp", sh, S["mod"][l, o:o + D].partition_broadcast(128), w=["sh"])
    P.dma("sp", scp, S["mod"][l, o + D:o + 2 * D].partition_broadcast(128), w=["scp"])
    P.op("dve", lambda e: e.tensor_scalar(out=scp, in0=scp, scalar1=1.0, scalar2=None, op0=ALU.add),
         r=["scp"], w=["scp"])
    for tb in range(T // 128):
        b = tb % 2
        x = xt[b]
        P.dma("sp", x, xsrc[tb * 128:(tb + 1) * 128, :], w=[("x", b)])
        P.op("dve", lambda e, x=x: e.tensor_tensor(out=x, in0=x, in1=scp, op=ALU.mult),
             r=[("x", b), "scp"], w=[("x", b)])
        P.op("pool", lambda e, x=x: e.tensor_tensor(out=x, in0=x, in1=sh, op=ALU.add),
             r=[("x", b), "sh"], w=[("x", b)])
        for g in range(KC // 4):
            pb = g % 2
            for j in range(4):
                fc = g * 4 + j
                P.op("pe", lambda e, x=x, fc=fc, pb=pb, j=j: e.transpose(
                    out=PS[pb][:, j * 128:(j + 1) * 128], in_=x[:, fc * 128:(fc + 1) * 128], identity=ident),
                    r=[("x", b), "ident"], w=[("pst", pb)])
            dst = ub[b][:, g * 4:(g + 1) * 4, :]
            src = PS[pb][:, :].rearrange("p (a b) -> p a b", b=128)
            if g % 2 == 0:
                P.op("act", lambda e, dst=dst, src=src: e.copy(out=dst, in_=src), r=[("pst", pb)], w=[("ub", b)])
            else:
                P.op("dve", lambda e, dst=dst, src=src: e.tensor_copy(out=dst, in_=src), r=[("pst", pb)],
                     w=[("ub", b)])
        P.dma("sp", S["uT"][:, tb * 128:(tb + 1) * 128].rearrange("(kc p) t -> p kc t", p=128), ub[b],
              r=[("ub", b)], w=[])


def linear(k, xT, tiles, Kdim, wfn, segs, wq="pool"):
    P, sb, PS = k.P, k.sb, k.PS
    kc_n = Kdim // 128
    nt = len(tiles)
    XT = sb.bf16(kc_n * nt * 512).rearrange("p (a b) -> p a b", b=nt * 512)
    WB = [sb.bf16(kc_n * 512).rearrange("p (a b) -> p a b", b=512) for _ in range(2)]
    OB = [sb.f32(512) for _ in range(4)]
    for i, t in enumerate(tiles):
        P.dma("sp", XT[:, :, i * 512:(i + 1) * 512],
              xT[:, t * 512:(t + 1) * 512].rearrange("(kc p) t -> p kc t", p=128),
              r=["xT_dram_in"], w=[("XT", i)])
    k.lin_ctr = getattr(k, "lin_ctr", 0)
    for sg in segs:
        c0, c1 = sg["c0"], sg["c1"]
        odt = sg.get("odt", BF16)
        for cb0 in range(c0, c1, 512):
            ncb = min(512, c1 - cb0)
            wbi = k.lin_ctr % 2
            k.lin_ctr += 1
            W = WB[wbi]
            P.dma(wq, W[:, :, 0:ncb], wfn(cb0, cb0 + ncb).rearrange("(kc p) n -> p kc n", p=128),
                  w=[("WB", wbi)])
            sel = [i for i, t in enumerate(tiles) if t in sg["tiles"]]
            if sg["mode"] == "F":
                for s0 in range(0, ncb, 128):
                    m = min(128, ncb - s0)
                    for i in sel:
                        k.ps_ctr = getattr(k, "ps_ctr", 0) + 1
                        pi = k.ps_ctr % 4
                        ps = PS[pi][0:m, :]
                        for kc in range(kc_n):
                            P.op("pe", lambda e, ps=ps, W=W, kc=kc, s0=s0, m=m, i=i: e.matmul(
                                ps, W[:, kc, s0:s0 + m], XT[:, kc, i * 512:(i + 1) * 512],
                                start=(kc == 0), stop=(kc == kc_n - 1)),
                                r=[("WB", wbi), ("XT", i)], w=[("ps", pi)])
                        ob = OB[pi]
                        obv = (ob if odt == F32 else ob.bitcast(BF16)[:, 0:512])[0:m, :]
                        func = sg.get("func", AF.Copy)
                        scale = sg.get("scale", 1.0)
                        if sg.get("bias") is not None:
                            bcol = sg["bias"][0:m, (cb0 - c0 + s0) // 128:(cb0 - c0 + s0) // 128 + 1]
                            P.op("act", lambda e, obv=obv, ps=ps, func=func, scale=scale, bcol=bcol: e.activation(
                                out=obv, in_=ps, func=func, bias=bcol, scale=scale),
                                r=[("ps", pi), "bias"], w=[("ob", pi)])
                        elif func == AF.Copy and k.ps_ctr % 2 == 0:
                            P.op("dve", lambda e, obv=obv, ps=ps, scale=scale: e.tensor_scalar(
                                out=obv, in0=ps, scalar1=float(scale), scalar2=None, op0=ALU.mult),
                                r=[("ps", pi)], w=[("ob", pi)])
                        else:
                            P.op("act", lambda e, obv=obv, ps=ps, func=func, scale=scale: e.activation(
                                out=obv, in_=ps, func=func, scale=scale),
                                r=[("ps", pi)], w=[("ob", pi)])
                        t = tiles[i]
                        P.dma("sp", sg["out"][cb0 - c0 + s0:cb0 - c0 + s0 + m, t * 512:(t + 1) * 512], obv,
                              r=[("ob", pi)], w=[])
            else:
                for i in sel:
                    for tb in range(4):
                        k.ps_ctr = getattr(k, "ps_ctr", 0) + 1
                        pi = k.ps_ctr % 4
                        ps = PS[pi][:, 0:ncb]
                        for kc in range(kc_n):
                            P.op("pe", lambda e, ps=ps, W=W, kc=kc, i=i, tb=tb, ncb=ncb: e.matmul(
                                ps, XT[:, kc, i * 512 + tb * 128:i * 512 + (tb + 1) * 128], W[:, kc, 0:ncb],
                                start=(kc == 0), stop=(kc == kc_n - 1)),
                                r=[("WB", wbi), ("XT", i)], w=[("ps", pi)])
                        ob = OB[pi]
                        obv = (ob if odt == F32 else ob.bitcast(BF16)[:, 0:512])[:, 0:ncb]
                        func = sg.get("func", AF.Copy)
                        if func == AF.Copy and k.ps_ctr % 2 == 0:
                            P.op("dve", lambda e, obv=obv, ps=ps: e.tensor_copy(out=obv, in_=ps),
                                 r=[("ps", pi)], w=[("ob", pi)])
                        else:
                            P.op("act", lambda e, obv=obv, ps=ps, func=func: e.activation(out=obv, in_=ps, func=func),
                                 r=[("ps", pi)], w=[("ob", pi)])
                        t = tiles[i]
                        r0 = t * 512 + tb * 128
                        P.dma("sp", sg["out"][r0:r0 + 128, cb0 - c0:cb0 - c0 + ncb], obv,
                              r=[("ob", pi)], w=[])


def stage_win(k, l, own_tiles):
    P, sb, I, S = k.P, k.sb, k.I, k.S
    for grp in ((0, 1), (2, 3)):
        sb.reset()
        segs = []
        for name, (c0, c1) in SEG.items():
            tl = [t for t in grp if (name not in Q_SEGS or t in own_tiles)]
            if not tl:
                continue
            sg = dict(name=name, c0=c0, c1=c1, mode="F" if name in F_SEGS else "T", out=S[name], tiles=tl)
            if name in ("qa", "foxq", "dilq"):
                sg["scale"] = HD ** -0.5
            if name in ("gates", "forget"):
                sg["odt"] = F32
            if name == "gates":
                sg["func"] = AF.Sigmoid
            segs.append(sg)
        linear(k, S["uT"], grp, D, lambda c0, c1: I["w_in"][l, :, c0:c1], segs)
        P.barrier()


TAB = dict(Dc=(0, 2432), D511=(2432, 1408), D128=(3840, 1024), Ddil4=(4864, 1408), Ddil16=(6272, 2432),
           Fox01=(8704, 2432))
TABW = 11136
HUGE = 1.0e9
MASKV = -30000.0


def alibi_slopes(n=24):
    return (np.float32(2.0) ** (-8.0 * np.arange(1, n + 1, dtype=np.float32) / n)).astype(np.float32)


def host_tables(h):
    shift = 512 if h == 0 else 0
    p = np.arange(128)[:, None]
    tabs = np.zeros((128, TABW), np.float32)

    def fill(name, fn):
        o, w = TAB[name]
        r = np.arange(w)[None, :] - 384
        dist = r - p
        tabs[:, o:o + w] = fn(dist)
    fill("Dc", lambda d: np.where(d >= 0, d, HUGE))
    fill("D511", lambda d: np.where((d >= 0) & (d <= 511), d, HUGE))
    fill("D128", lambda d: np.where((d >= 0) & (d <= 128), d, HUGE))
    fill("Ddil4", lambda d: np.where((d >= 0) & (d <= 512) & (d % 4 == 0), d, HUGE))
    fill("Ddil16", lambda d: np.where((d >= 0) & (d % 16 == 0), d, HUGE))
    fill("Fox01", lambda d: np.where(d >= 0, 0.0, MASKV))
    kbias = np.zeros((128, 16), np.float32)
    kbias[:, :shift // 128] = MASKV
    n = np.arange(128)[:, None]
    q = np.arange(T)[None, :]
    dc = q - (16 * n + 31)
    okc = (dc >= 0) & (16 * n >= shift) & (n <= 126)
    dcmp = np.where(okc, dc, HUGE).astype(np.float32)
    cm = np.zeros((128, 32), np.float32)
    for j in range(32):
        for m in range(4):
            for nn in range(2):
                i = 4 * j + m + nn
                if i < 127:
                    cm[i, j] += 1.0
    E = np.zeros((32, T), np.float32)
    E[np.arange(T) // 64, np.arange(T)] = 1.0
    g = np.arange(T) - shift
    jb = np.arange(32)[None, :] - shift // 64
    qblk = (g // 64)[:, None]
    forced = ((jb == 0) | (jb == qblk) | (jb == qblk - 1)) & (g[:, None] >= 0)
    causal = (jb >= 0) & (jb <= qblk) & (g[:, None] >= 0)
    smul = (causal & ~forced).astype(np.float32)
    sadd = np.where(forced, 1e30, np.where(causal, 0.0, -1e30)).astype(np.float32)
    tri = np.zeros((128, 5, 512), np.float32)
    s = np.arange(128)[:, None]
    t = np.arange(512)[None, :]
    for d in range(4):
        tri[:, d, :] = (128 * d + s <= t)
    tri[:, 4, :] = 1.0
    import ml_dtypes
    return dict(tabs=tabs, kbias=kbias, dcmp=dcmp, cm=cm, Emat=E.astype(ml_dtypes.bfloat16), smul=smul, sadd=sadd,
                tri=tri, ident=np.eye(128, dtype=np.float32))


def bc_ap(ap, axis, n):
    lay = [list(x) for x in ap.ap]
    lay.insert(1 + axis, [0, n])
    return bass.AP(ap.tensor, ap.offset, lay)


class Attn:
    def __init__(self, k, own_tiles):
        self.k = k
        self.own = own_tiles
        P, sb, PS, I, S = k.P, k.sb, k.PS, k.I, k.S
        sb.reset()
        self.tabs = sb.f32(TABW)
        self.kbias = sb.f32(16)
        self.ident = sb.f32(128)
        P.dma("sp", self.tabs, I["tabs"][:, :], w=["tabs"])
        P.dma("sp", self.kbias, I["kbias"][:, :], w=["kbias"])
        P.dma("sp", self.ident, I["ident"][:, :], w=["ident"])
        self.KT = [sb.bf16(T) for _ in range(2)]
        self.V = [sb.bf16(16 * 129).rearrange("p (a b) -> p a b", b=129) for _ in range(2)]
        for i in range(2):
            P.op("pool", lambda e, v=self.V[i]: e.memset(v, 1.0), w=[("V", i)])
        self.QT = [sb.bf16(512) for _ in range(2)]
        self.Ssb = [sb.f32(512) for _ in range(3)]
        self.PT = [sb.bf16(512) for _ in range(3)]
        self.kv_ctr = 0
        self.q_ctr = 0
        self.s_ctr = 0
        self.acc_ctr = 0
        self.sm = sb.f32(64)
        self.sm_ctr = 0
        self.ACC = k.ACC

    def load_kv(self, kt_dram_rows, v_dram_cols):
        P = self.k.P
        i = self.kv_ctr % 2
        self.kv_ctr += 1
        P.dma("sp", self.KT[i], kt_dram_rows, w=[("KT", i)])
        if v_dram_cols is not None:
            P.dma("sp", self.V[i][:, :, 0:128], v_dram_cols.rearrange("(b p) d -> p b d", p=128), w=[("V", i)])
        return i

    def load_q(self, q_dram):
        P = self.k.P
        i = self.q_ctr % 2
        self.q_ctr += 1
        P.dma("sp", self.QT[i], q_dram, w=[("QT", i)])
        return i

    def tiles(self, qi, kvi, tl, kblks, tabname, scalar, aug=None, aug_keys=()):
        k, P, PS = self.k, self.k.P, self.k.PS
        ai = self.acc_ctr % 2
        self.acc_ctr += 1
        acc = self.ACC[ai]
        o, w = TAB[tabname]
        first = True
        for n_i, kb in enumerate(kblks):
            si = self.s_ctr % 2
            bi = self.s_ctr % 3
            self.s_ctr += 1
            ps = PS[si]
            KTs = self.KT[kvi][:, kb * 128:(kb + 1) * 128]
            QTt = self.QT[qi]
            P.op("pe", lambda e, ps=ps, KTs=KTs, QTt=QTt, aug=aug: e.matmul(ps, KTs, QTt, start=True,
                                                                           stop=(aug is None)),
                 r=[("KT", kvi), ("QT", qi)], w=[("S", si)])
            if aug is not None:
                lhs, rhs = aug(kb)
                P.op("pe", lambda e, ps=ps, lhs=lhs, rhs=rhs: e.matmul(ps, lhs, rhs, start=False, stop=True),
                     r=list(aug_keys), w=[("S", si)])
            c0 = o + 512 * tl - 128 * kb + 384
            assert c0 >= o and c0 + 512 <= o + w, (tabname, tl, kb)
            tsl = self.tabs[:, c0:c0 + 512]
            ssb = self.Ssb[bi]
            P.op("dve", lambda e, ssb=ssb, tsl=tsl, ps=ps, scalar=scalar: e.scalar_tensor_tensor(
                out=ssb, in0=tsl, scalar=float(scalar), in1=ps, op0=ALU.mult, op1=ALU.add),
                r=[("S", si), "tabs"], w=[("Ssb", bi)])
            pt = self.PT[bi]
            kbc = self.kbias[:, kb:kb + 1]
            P.op("act", lambda e, pt=pt, ssb=ssb, kbc=kbc: e.activation(out=pt, in_=ssb, func=AF.Exp, bias=kbc),
                 r=[("Ssb", bi), "kbias"], w=[("PT", bi)])
            Vb = self.V[kvi][:, kb, :]
            for qb in range(4):
                dst = acc[:, qb // 2, (qb % 2) * 129:(qb % 2) * 129 + 129]
                st = first and (qb % 2 == 0)
                P.op("pe", lambda e, dst=dst, pt=pt, qb=qb, Vb=Vb, st=st: e.matmul(
                    dst, pt[:, qb * 128:(qb + 1) * 128], Vb, start=st, stop=True, skip_group_check=True),
                    r=[("PT", bi), ("V", kvi)], w=[("ACC", ai)])
            first = False
        return ai

    def rinv(self, ai, ncols=129, gate=None):
        P = self.k.P
        acc = self.ACC[ai]
        j = self.sm_ctr % 8
        self.sm_ctr += 1
        r = self.sm[:, j * 4:(j + 1) * 4]
        rs = acc[:, :, 0:2 * ncols].rearrange("p b (q c) -> p b q c", c=ncols)[:, :, :, ncols - 1]
        r3 = r.rearrange("p (b q) -> p b q", q=2)
        P.op("dve", lambda e, r3=r3, rs=rs: e.tensor_scalar(out=r3, in0=rs, scalar1=1e-30, scalar2=None, op0=ALU.max),
             r=[("ACC", ai)], w=[("sm", j)])
        P.op("dve", lambda e, r=r: e.reciprocal(out=r, in_=r), r=[("sm", j)], w=[("sm", j)])
        if gate is not None:
            P.op("dve", lambda e, r=r, gate=gate: e.tensor_tensor(out=r, in0=r, in1=gate, op=ALU.mult),
                 r=[("sm", j), "gates"], w=[("sm", j)])
        return r, ("sm", j)

    def scale_into(self, ai, r, rkey, dst, dkey, accumulate, ncols=129):
        P = self.k.P
        acc = self.ACC[ai]
        for qb in range(4):
            src = acc[:, qb // 2, (qb % 2) * ncols:(qb % 2) * ncols + 128]
            d = dst[:, qb, :]
            rc = r[:, qb:qb + 1]
            if accumulate:
                P.op("dve", lambda e, d=d, src=src, rc=rc: e.scalar_tensor_tensor(
                    out=d, in0=src, scalar=rc, in1=d, op0=ALU.mult, op1=ALU.add),
                    r=[("ACC", ai), rkey, dkey], w=[dkey])
            else:
                P.op("dve", lambda e, d=d, src=src, rc=rc: e.tensor_scalar(
                    out=d, in0=src, scalar1=rc, scalar2=None, op0=ALU.mult),
                    r=[("ACC", ai), rkey], w=[dkey])

    def store_oT(self, src, skey, row0, tl, obuf, okey):
        k, P, PS = self.k, self.k.P, self.k.PS
        tb = 6 + (self.acc_ctr % 2)
        self.acc_ctr += 0
        ps = PS[tb]
        for qb in range(4):
            P.op("pe", lambda e, ps=ps, src=src, qb=qb: e.transpose(out=ps[:, qb * 128:(qb + 1) * 128],
                                                                   in_=src[:, qb, :], identity=self.ident),
                 r=[skey, "ident"], w=[("pst", tb)])
        P.op("act", lambda e, obuf=obuf, ps=ps: e.copy(out=obuf, in_=ps), r=[("pst", tb)], w=[okey])
        P.dma("sp", k.S["oT"][row0:row0 + 128, tl * 512:(tl + 1) * 512], obuf, r=[okey], w=[])


GELU = AF.Gelu_apprx_tanh


def stage_cmp(k, l):
    P, sb, PS, I, S = k.P, k.sb, k.PS, k.I, k.S
    sb.reset()
    NBK = 127
    for i, (src, ) in enumerate((("kcmp",), ("vcmp",))):
        W1 = sb.bf16(32 * 128).rearrange("p (a b) -> p a b", b=128)
        W2 = sb.bf16(128)
        peT = sb.bf16(32)
        hb = sb.f32(1)
        P.dma("pool", W1, I["cmp_w1"][l, i].rearrange("(a p) o -> p a o", p=128), w=[("W1", i)])
        P.dma("pool", W2, I["cmp_w2"][l, i], w=[("W2", i)])
        P.dma("pool", peT, I["cmp_peT"][l, i], w=[("peT", i)])
        psb = PS[0][:, 0:1]
        for a in range(32):
            P.op("pe", lambda e, a=a, W1=W1, peT=peT, psb=psb: e.matmul(psb, W1[:, a, :], peT[:, a:a + 1],
                                                                       start=(a == 0), stop=(a == 31)),
                 r=[("W1", i), ("peT", i)], w=[("psb", i)])
        P.op("dve", lambda e, hb=hb, psb=psb: e.tensor_copy(out=hb, in_=psb), r=[("psb", i)], w=[("hb", i)])
        for kv in range(3):
            aT = sb.bf16(T)
            hm = sb.bf16(128)
            P.dma("sp", aT, S[src][kv * 128:(kv + 1) * 128, :], w=[("aT", i, kv)])
            P.op("pool", lambda e, hm=hm: e.memset(hm, 0.0), w=[("hm", i, kv)])
            ps = PS[1 + (kv % 2)][:, 0:NBK]
            for a in range(32):
                rhs = aT[:, a:a + 16 * (NBK - 1) + 1:16]
                P.op("pe", lambda e, ps=ps, a=a, W1=W1, rhs=rhs: e.matmul(ps, W1[:, a, :], rhs, start=(a == 0),
                                                                         stop=(a == 31)),
                     r=[("W1", i), ("aT", i, kv)], w=[("psh", kv % 2)])
            P.op("act", lambda e, hm=hm, ps=ps, hb=hb: e.activation(out=hm[:, 0:NBK], in_=ps, func=GELU, bias=hb),
                 r=[("psh", kv % 2), ("hb", i), ("hm", i, kv)], w=[("hm", i, kv)])
            po = PS[3 + (kv % 2)][:, 0:128]
            if i == 0:
                P.op("pe", lambda e, po=po, W2=W2, hm=hm: e.matmul(po, W2, hm, start=True, stop=True),
                     r=[("W2", i), ("hm", i, kv)], w=[("pso", kv % 2)])
                ob = sb.bf16(128)
                P.op("dve", lambda e, ob=ob, po=po: e.tensor_copy(out=ob, in_=po), r=[("pso", kv % 2)],
                     w=[("cob", i, kv)])
                P.dma("sp", S["kc"][kv], ob, r=[("cob", i, kv)], w=[])
            else:
                P.op("pe", lambda e, po=po, W2=W2, hm=hm: e.matmul(po, hm, W2, start=True, stop=True),
                     r=[("W2", i), ("hm", i, kv)], w=[("pso", kv % 2)])
                ob = sb.f32(128)
                P.op("dve", lambda e, ob=ob, po=po: e.tensor_copy(out=ob, in_=po), r=[("pso", kv % 2)],
                     w=[("cob", i, kv)])
                P.dma("sp", S["vc"][kv], ob, r=[("cob", i, kv)], w=[])


def stage_cumsum(k, l):
    P, sb, PS, I, S = k.P, k.sb, k.PS, k.I, k.S
    sb.reset()
    ft = sb.f32(T, parts=8)
    nF = sb.f32(T, parts=8)
    Fr = sb.f32(T, parts=8)
    ones = sb.f32(T, parts=8)
    nb = sb.f32(1, parts=8)
    P.dma("sp", ft, S["forget"], w=["ft"])
    P.dma("sp", nb, I["b_forget"][l, :].rearrange("(h o) -> h o", o=1), w=["nb"])
    P.op("dve", lambda e: e.tensor_scalar(out=nb, in0=nb, scalar1=-1.0, scalar2=None, op0=ALU.mult), r=["nb"], w=["nb"])
    P.op("pool", lambda e: e.memset(ones, 1.0), w=["ones"])
    P.op("act", lambda e: e.activation(out=ft, in_=ft, func=AF.Exp, scale=-1.0, bias=nb), r=["ft", "nb"], w=["ft"])
    P.op("dve", lambda e: e.tensor_scalar(out=ft, in0=ft, scalar1=1.0, scalar2=None, op0=ALU.add), r=["ft"], w=["ft"])
    P.op("act", lambda e: e.activation(out=ft, in_=ft, func=AF.Ln), r=["ft"], w=["ft"])
    P.op("dve", lambda e: e.tensor_tensor_scan(out=nF, data0=ones, data1=ft, initial=0.0, op0=ALU.mult, op1=ALU.add),
         r=["ft", "ones"], w=["nF"])
    P.op("dve", lambda e: e.tensor_scalar(out=Fr, in0=nF, scalar1=-1.0, scalar2=None, op0=ALU.mult), r=["nF"], w=["Fr"])
    P.dma("sp", S["Frow"], Fr, r=["Fr"], w=[])
    P.dma("sp", S["nFrow"], nF, r=["nF"], w=[])


def stage_nsa(k, l, own):
    P, sb, PS, I, S = k.P, k.sb, k.PS, k.I, k.S
    A = Attn(k, own)
    sl = alibi_slopes()
    dcmp = sb.f32(T)
    P.dma("sp", dcmp, I["dcmp"], w=["dcmp"])
    Esb = sb.bf16(T, parts=32)
    P.dma("sp", Esb, I["Emat"], w=["E"])
    rhs161 = [sb.f32(161) for _ in range(2)]
    kcT = [sb.bf16(128) for _ in range(2)]
    Ef = [sb.f32(512) for _ in range(2)]
    ACCW = [PS[2 + 2 * i] for i in range(2)]
    oacc = {(hh, tl): sb.f32(512).rearrange("p (q d) -> p q d", d=128) for hh in range(4) for tl in own}
    pslc = {tl: sb.f32(128).rearrange("p (q j) -> p q j", j=32) for tl in own}
    gat = {tl: sb.f32(4 * 36).rearrange("p (q c) -> p q c", c=36) for tl in own}
    smul = {tl: sb.f32(128).rearrange("p (q j) -> p q j", j=32) for tl in own}
    sadd = {tl: sb.f32(128).rearrange("p (q j) -> p q j", j=32) for tl in own}
    selT = {tl: sb.bf16(512, parts=32) for tl in own}
    sc = sb.f32(128).rearrange("p (q j) -> p q j", j=32)
    sc2 = sb.f32(128).rearrange("p (q j) -> p q j", j=32)
    m8 = sb.f32(64).rearrange("p (q j) -> p q j", j=16)
    obuf = [sb.bf16(512) for _ in range(2)]
    for tl in own:
        tsl = slice(tl * 512, (tl + 1) * 512)
        P.dma("sp", gat[tl], S["gates"][tsl, :].rearrange("(q p) c -> p q c", p=128), w=["gates"])
        P.dma("sp", smul[tl], I["smul"][tsl, :].rearrange("(q p) c -> p q c", p=128), w=[("smul", tl)])
        P.dma("sp", sadd[tl], I["sadd"][tsl, :].rearrange("(q p) c -> p q c", p=128), w=[("sadd", tl)])
    cmpc = 0
    for kv in range(3):
        b = kv % 2
        P.op("pool", lambda e, b=b: e.memset(rhs161[b], 1.0), w=[("rhs161", b)])
        P.dma("sp", rhs161[b][:, 0:128], S["vc"][kv], w=[("rhs161", b)])
        P.dma("sp", rhs161[b][:, 128:160], I["cm"], w=[("rhs161", b)])
        P.dma("sp", kcT[b], S["kc"][kv], w=[("kcT", b)])
        for hh in range(4):
            hq = kv * 4 + hh
            slope = float(sl[2 * hq])
            for tl in own:
                qi = A.load_q(S["qa"][hq * 128:(hq + 1) * 128, tl * 512:(tl + 1) * 512])
                si = A.s_ctr % 2
                bi = A.s_ctr % 3
                A.s_ctr += 1
                ps = PS[si]
                P.op("pe", lambda e, ps=ps, b=b, qi=qi: e.matmul(ps, kcT[b], A.QT[qi], start=True, stop=True),
                     r=[("kcT", b), ("QT", qi)], w=[("S", si)])
                ssb = A.Ssb[bi]
                dsl = dcmp[:, tl * 512:(tl + 1) * 512]
                P.op("dve", lambda e, ssb=ssb, dsl=dsl, ps=ps, slope=slope: e.scalar_tensor_tensor(
                    out=ssb, in0=dsl, scalar=-slope, in1=ps, op0=ALU.mult, op1=ALU.add),
                    r=[("S", si), "dcmp"], w=[("Ssb", bi)])
                ef = Ef[cmpc % 2]
                ek = ("Ef", cmpc % 2)
                cmpc += 1
                P.op("act", lambda e, ef=ef, ssb=ssb: e.activation(out=ef, in_=ssb, func=AF.Exp),
                     r=[("Ssb", bi)], w=[ek])
                ai = A.acc_ctr % 2
                A.acc_ctr += 1
                acc = A.ACC[ai]
                for qb in range(4):
                    dst = acc[:, qb // 2, (qb % 2) * 161:(qb % 2) * 161 + 161]
                    P.op("pe", lambda e, dst=dst, ef=ef, qb=qb, b=b: e.matmul(
                        dst, ef[:, qb * 128:(qb + 1) * 128], rhs161[b], start=(qb % 2 == 0), stop=True,
                        skip_group_check=True), r=[ek, ("rhs161", b)], w=[("ACC", ai)])
                r0, r0k = A.rinv(ai, ncols=161)
                for qb in range(4):
                    src = acc[:, qb // 2, (qb % 2) * 161 + 128:(qb % 2) * 161 + 160]
                    d = pslc[tl][:, qb, :]
                    rc = r0[:, qb:qb + 1]
                    if hh == 0:
                        P.op("dve", lambda e, d=d, src=src, rc=rc: e.tensor_scalar(
                            out=d, in0=src, scalar1=rc, scalar2=None, op0=ALU.mult),
                            r=[("ACC", ai), r0k], w=[("pslc", tl)])
                    else:
                        P.op("dve", lambda e, d=d, src=src, rc=rc: e.scalar_tensor_tensor(
                            out=d, in0=src, scalar=rc, in1=d, op0=ALU.mult, op1=ALU.add),
                            r=[("ACC", ai), r0k, ("pslc", tl)], w=[("pslc", tl)])
                g0 = gat[tl][:, :, hq * 3 + 0]
                P.op("dve", lambda e, r0=r0, g0=g0: e.tensor_tensor(out=r0, in0=r0, in1=g0, op=ALU.mult),
                     r=[r0k, "gates"], w=[r0k])
                A.scale_into(ai, r0, r0k, oacc[(hh, tl)], ("oacc", hh, tl), accumulate=False, ncols=161)
        for tl in own:
            P.op("dve", lambda e, tl=tl: e.tensor_tensor(out=sc, in0=pslc[tl], in1=smul[tl], op=ALU.mult),
                 r=[("pslc", tl), ("smul", tl)], w=["sc"])
            P.op("dve", lambda e, tl=tl: e.tensor_tensor(out=sc, in0=sc, in1=sadd[tl], op=ALU.add),
                 r=["sc", ("sadd", tl)], w=["sc"])
            for qb in range(4):
                P.op("dve", lambda e, qb=qb: e.max(out=m8[:, qb, 0:8], in_=sc[:, qb, :]), r=["sc"], w=["m8"])
                P.op("dve", lambda e, qb=qb: e.match_replace(out=sc2[:, qb, :], in_to_replace=m8[:, qb, 0:8],
                                                             in_values=sc[:, qb, :], imm_value=-3.0e38),
                     r=["sc", "m8"], w=["sc2"])
                P.op("dve", lambda e, qb=qb: e.max(out=m8[:, qb, 8:16], in_=sc2[:, qb, :]), r=["sc2"], w=["m8"])
                P.op("dve", lambda e, qb=qb: e.tensor_scalar(out=sc2[:, qb, :], in0=sc[:, qb, :],
                                                             scalar1=m8[:, qb, 15:16], scalar2=None, op0=ALU.is_ge),
                     r=["sc", "m8", "sc2"], w=["sc2"])
            P.op("dve", lambda e: e.tensor_scalar(out=sc2, in0=sc2, scalar1=-1.0, scalar2=-MASKV, op0=ALU.add,
                                                  op1=ALU.mult), r=["sc2"], w=["sc2"])
            pst = PS[6][0:32, :]
            for qb in range(4):
                P.op("pe", lambda e, qb=qb, pst=pst: e.transpose(out=pst[:, qb * 128:(qb + 1) * 128],
                                                                in_=sc2[:, qb, :], identity=A.ident),
                     r=["sc2", "ident"], w=[("pst", 6)])
            P.op("act", lambda e, tl=tl, pst=pst: e.copy(out=selT[tl], in_=pst), r=[("pst", 6)], w=[("selT", tl)])
        for br, (ks, vs, tabn) in enumerate((("kslc", "vslc", "Dc"), ("kwin", "vwin", "D511"))):
            kvi = A.load_kv(S[ks][kv * 128:(kv + 1) * 128, :], S[vs][:, kv * 128:(kv + 1) * 128])
            for hh in range(4):
                hq = kv * 4 + hh
                slope = float(sl[2 * hq])
                for tl in own:
                    qi = A.load_q(S["qa"][hq * 128:(hq + 1) * 128, tl * 512:(tl + 1) * 512])
                    if br == 0:
                        kbl = list(range(0, 4 * tl + 4))
                        aug = (lambda kb, tl=tl: (Esb[:, kb * 128:(kb + 1) * 128], selT[tl]))
                        ai = A.tiles(qi, kvi, tl, kbl, tabn, -slope, aug=aug, aug_keys=["E", ("selT", tl)])
                    else:
                        kbl = list(range(max(0, 4 * tl - 4), 4 * tl + 4))
                        ai = A.tiles(qi, kvi, tl, kbl, tabn, -slope)
                    r, rk = A.rinv(ai, gate=gat[tl][:, :, hq * 3 + 1 + br])
                    A.scale_into(ai, r, rk, oacc[(hh, tl)], ("oacc", hh, tl), accumulate=True)
        for hh in range(4):
            hq = kv * 4 + hh
            for tl in own:
                ob = obuf[(hh + tl) % 2]
                A.store_oT(oacc[(hh, tl)], ("oacc", hh, tl), hq * 128, tl, ob, ("obuf", (hh + tl) % 2))


def stage_fox(k, l, own):
    P, sb, PS, I, S = k.P, k.sb, k.PS, k.I, k.S
    A = Attn(k, own)
    FAq = [sb.f32(T, parts=2) for _ in range(2)]
    FAk = [sb.f32(T, parts=2) for _ in range(2)]
    ob32 = [sb.f32(512).rearrange("p (q d) -> p q d", d=128) for _ in range(2)]
    obuf = [sb.bf16(512) for _ in range(2)]
    for i in range(2):
        P.op("pool", lambda e, i=i: e.memset(FAq[i], 1.0), w=[("FA", i)])
        P.op("pool", lambda e, i=i: e.memset(FAk[i], 1.0), w=[("FA", i)])
    c = 0
    for hd in range(8):
        kvi = A.load_kv(S["foxk"][hd * 128:(hd + 1) * 128, :], S["foxv"][:, hd * 128:(hd + 1) * 128])
        fb = hd % 2
        P.dma("sp", FAq[fb][0:1, :], S["Frow"][hd:hd + 1, :], w=[("FA", fb)])
        P.dma("sp", FAk[fb][1:2, :], S["nFrow"][hd:hd + 1, :], w=[("FA", fb)])
        for tl in own:
            qi = A.load_q(S["foxq"][hd * 128:(hd + 1) * 128, tl * 512:(tl + 1) * 512])
            aug = (lambda kb, tl=tl, fb=fb: (FAk[fb][:, kb * 128:(kb + 1) * 128], FAq[fb][:, tl * 512:(tl + 1) * 512]))
            ai = A.tiles(qi, kvi, tl, list(range(0, 4 * tl + 4)), "Fox01", 1.0, aug=aug, aug_keys=[("FA", fb)])
            r, rk = A.rinv(ai)
            ob = ob32[c % 2]
            A.scale_into(ai, r, rk, ob, ("ob32", c % 2), accumulate=False)
            A.store_oT(ob, ("ob32", c % 2), 1536 + hd * 128, tl, obuf[c % 2], ("obuf", c % 2))
            c += 1


def stage_dil(k, l, own):
    P, sb, PS, I, S = k.P, k.sb, k.PS, k.I, k.S
    A = Attn(k, own)
    sl = alibi_slopes()
    oraw = {tl: sb.f32(4 * 129).rearrange("p (q d) -> p q d", d=129) for tl in own}
    ob32 = [sb.f32(512).rearrange("p (q d) -> p q d", d=128) for _ in range(2)]
    obuf = [sb.bf16(512) for _ in range(2)]
    rr = sb.f32(8)
    tabn = ["D128", "Ddil4", "Ddil16"]
    c = 0
    for j in range(4):
        for g in range(3):
            hx = g * 4 + j
            slope = float(sl[2 * hx + 1])
            kvi = A.load_kv(S["dilk"][hx * 128:(hx + 1) * 128, :], S["dilv"][:, hx * 128:(hx + 1) * 128])
            for tl in own:
                qi = A.load_q(S["dilq"][hx * 128:(hx + 1) * 128, tl * 512:(tl + 1) * 512])
                if g == 0:
                    kbl = list(range(max(0, 4 * tl - 1), 4 * tl + 4))
                elif g == 1:
                    kbl = list(range(max(0, 4 * tl - 4), 4 * tl + 4))
                else:
                    kbl = list(range(0, 4 * tl + 4))
                ai = A.tiles(qi, kvi, tl, kbl, tabn[g], -slope)
                acc = A.ACC[ai]
                for bk in range(2):
                    src = acc[:, bk, 0:258]
                    d = oraw[tl][:, 2 * bk:2 * bk + 2, :].rearrange("p q d -> p (q d)")
                    if g == 0:
                        P.op("dve", lambda e, d=d, src=src: e.tensor_copy(out=d, in_=src), r=[("ACC", ai)],
                             w=[("oraw", tl)])
                    else:
                        P.op("dve", lambda e, d=d, src=src: e.tensor_tensor(out=d, in0=d, in1=src, op=ALU.add),
                             r=[("ACC", ai), ("oraw", tl)], w=[("oraw", tl)])
        for tl in own:
            r = rr[:, (c % 2) * 4:(c % 2) * 4 + 4]
            rk = ("rr", c % 2)
            P.op("dve", lambda e, r=r, tl=tl: e.tensor_scalar(out=r, in0=oraw[tl][:, :, 128], scalar1=1e-30,
                                                              scalar2=None, op0=ALU.max), r=[("oraw", tl)], w=[rk])
            P.op("dve", lambda e, r=r: e.reciprocal(out=r, in_=r), r=[rk], w=[rk])
            ob = ob32[c % 2]
            for qb in range(4):
                P.op("dve", lambda e, ob=ob, tl=tl, qb=qb, r=r: e.tensor_scalar(
                    out=ob[:, qb, :], in0=oraw[tl][:, qb, 0:128], scalar1=r[:, qb:qb + 1], scalar2=None,
                    op0=ALU.mult), r=[("oraw", tl), rk], w=[("ob32", c % 2)])
            A.store_oT(ob, ("ob32", c % 2), 2560 + j * 128, tl, obuf[c % 2], ("obuf", c % 2))
            c += 1


ALPHA = (2.0 * L) ** 0.25
LN_EPS = 1e-5


def own_groups(own):
    own = list(own)
    return [tuple(own[i:i + 2]) for i in range(0, len(own), 2)]


def stage_gates(k, l, own):
    P, sb, I, S = k.P, k.sb, k.I, k.S
    for grp in own_groups(own):
        sb.reset()
        bg = sb.f32(96)
        P.dma("sp", bg, I["b_gateT"][l], w=["bias"])
        sg = dict(name="g", c0=0, c1=3 * D, mode="F", out=S["gT"], tiles=list(grp), func=AF.Sigmoid, bias=bg)
        linear(k, S["uT"], grp, D, lambda c0, c1: I["w_gate"][l, :, c0:c1], [sg])
        P.barrier()


def stage_branch(k, l, own):
    P, sb, I, S = k.P, k.sb, k.I, k.S
    for grp in own_groups(own):
        for nm, r0, r1 in (("ya", 0, 1536), ("yb", 1536, 2560), ("yc", 2560, 3072)):
            sb.reset()
            sg = dict(name=nm, c0=0, c1=D, mode="F", out=S[nm], tiles=list(grp))
            linear(k, S["oT"][r0:r1, :], grp, r1 - r0, lambda c0, c1, r0=r0, r1=r1: I["w_branch"][l, r0:r1, c0:c1],
                   [sg])
            P.barrier()


def stage_merge(k, l, own):
    P, sb, I, S = k.P, k.sb, k.I, k.S
    sb.reset()
    NB = 3
    bufs = [[sb.bf16(512) for _ in range(6)] for _ in range(NB)]
    tmp = [[sb.f32(512) for _ in range(2)] for _ in range(NB)]
    outb = [sb.bf16(512) for _ in range(NB)]
    c = 0
    for fc in range(KC):
        for tl in own:
            b = c % NB
            c += 1
            ts = slice(tl * 512, (tl + 1) * 512)
            srcs = [S["gT"][fc * 128:(fc + 1) * 128, ts], S["gT"][D + fc * 128:D + (fc + 1) * 128, ts],
                    S["gT"][2 * D + fc * 128:2 * D + (fc + 1) * 128, ts],
                    S["ya"][fc * 128:(fc + 1) * 128, ts], S["yb"][fc * 128:(fc + 1) * 128, ts],
                    S["yc"][fc * 128:(fc + 1) * 128, ts]]
            for j in range(6):
                P.dma("sp" if j % 2 == 0 else "act", bufs[b][j], srcs[j], w=[("mb", b, j)])
            B = bufs[b]
            t0, t1 = tmp[b]
            P.op("dve", lambda e, B=B, t0=t0: e.tensor_tensor(out=t0, in0=B[0], in1=B[3], op=ALU.mult),
                 r=[("mb", b, 0), ("mb", b, 3)], w=[("mt0", b)])
            P.op("pool", lambda e, B=B, t1=t1: e.tensor_tensor(out=t1, in0=B[1], in1=B[4], op=ALU.mult),
                 r=[("mb", b, 1), ("mb", b, 4)], w=[("mt1", b)])
            P.op("dve", lambda e, t0=t0, t1=t1: e.tensor_tensor(out=t0, in0=t0, in1=t1, op=ALU.add),
                 r=[("mt0", b), ("mt1", b)], w=[("mt0", b)])
            P.op("pool", lambda e, B=B, t1=t1: e.tensor_tensor(out=t1, in0=B[2], in1=B[5], op=ALU.mult),
                 r=[("mb", b, 2), ("mb", b, 5), ("mt1", b)], w=[("mt1", b)])
            ob = outb[b]
            P.op("dve", lambda e, t0=t0, t1=t1, ob=ob: e.tensor_tensor(out=ob, in0=t0, in1=t1, op=ALU.add),
                 r=[("mt0", b), ("mt1", b)], w=[("mob", b)])
            P.dma("sp", S["mT"][fc * 128:(fc + 1) * 128, ts], ob, r=[("mob", b)], w=[])


def stage_out(k, l, own):
    P, sb, I, S = k.P, k.sb, k.I, k.S
    for grp in own_groups(own):
        sb.reset()
        sg = dict(name="y", c0=0, c1=D, mode="T", out=S["y"], tiles=list(grp), odt=F32)
        linear(k, S["mT"], grp, D, lambda c0, c1: I["w_out"][l, :, c0:c1], [sg])
        P.barrier()


def stage_ln(k, l, own, which, xsrc, ysrc, xdst, dst_map=None, modulate=True):
    P, sb, PS, I, S = k.P, k.sb, k.PS, k.I, k.S
    sb.reset()
    ident = sb.f32(128)
    gp = sb.f32(D)
    lg = sb.f32(D)
    lb = sb.f32(D)
    P.dma("sp", ident, I["ident"], w=["ident"])
    o = which * 3 * D
    P.dma("sp", gp, S["mod"][l, o + 2 * D:o + 3 * D].partition_broadcast(128), w=["gp"])
    P.op("dve", lambda e: e.tensor_scalar(out=gp, in0=gp, scalar1=1.0, scalar2=None, op0=ALU.add), r=["gp"], w=["gp"])
    P.dma("sp", lg, I["ln1_g" if which == 0 else "ln2_g"][l, :].partition_broadcast(128), w=["lg"])
    P.dma("sp", lb, I["ln1_b" if which == 0 else "ln2_b"][l, :].partition_broadcast(128), w=["lb"])
    if modulate:
        scp = sb.f32(D)
        sh = sb.f32(D)
        P.dma("sp", sh, S["mod"][l, 3 * D:4 * D].partition_broadcast(128), w=["sh"])
        P.dma("sp", scp, S["mod"][l, 4 * D:5 * D].partition_broadcast(128), w=["scp"])
        P.op("dve", lambda e: e.tensor_scalar(out=scp, in0=scp, scalar1=1.0, scalar2=None, op0=ALU.add),
             r=["scp"], w=["scp"])
        ub = [sb.bf16(KC * 128).rearrange("p (a b) -> p a b", b=128) for _ in range(2)]
    xt = [sb.f32(D) for _ in range(2)]
    yt = [sb.f32(D) for _ in range(2)]
    st = [sb.f32(8 * 6) for _ in range(2)]
    mv = [sb.f32(4) for _ in range(2)]
    c = 0
    for tl in own:
        for q in range(4):
            tb = tl * 4 + q
            b = c % 2
            c += 1
            x, y = xt[b], yt[b]
            P.dma("sp", x, xsrc[tb * 128:(tb + 1) * 128, :], w=[("x", b)])
            P.dma("act", y, ysrc[tb * 128:(tb + 1) * 128, :], w=[("y", b)])
            P.op("pool", lambda e, y=y: e.tensor_tensor(out=y, in0=y, in1=gp, op=ALU.mult), r=[("y", b), "gp"],
                 w=[("y", b)])
            P.op("dve", lambda e, x=x, y=y: e.scalar_tensor_tensor(out=y, in0=x, scalar=float(ALPHA), in1=y,
                                                                   op0=ALU.mult, op1=ALU.add),
                 r=[("x", b), ("y", b)], w=[("y", b)])
            s6 = st[b]
            for j in range(8):
                P.op("dve", lambda e, s6=s6, y=y, j=j: e.bn_stats(out=s6[:, j * 6:(j + 1) * 6],
                                                                 in_=y[:, j * 512:(j + 1) * 512]),
                     r=[("y", b)], w=[("st", b)])
            m = mv[b]
            P.op("dve", lambda e, m=m, s6=s6: e.bn_aggr(out=m[:, 0:2], in_=s6), r=[("st", b)], w=[("mv", b)])
            P.op("dve", lambda e, m=m: e.tensor_scalar(out=m[:, 2:3], in0=m[:, 1:2], scalar1=float(LN_EPS),
                                                       scalar2=None, op0=ALU.add), r=[("mv", b)], w=[("mv", b)])
            P.op("act", lambda e, m=m: e.activation(out=m[:, 2:3], in_=m[:, 2:3], func=AF.Sqrt), r=[("mv", b)],
                 w=[("mv", b)])
            P.op("dve", lambda e, m=m: e.reciprocal(out=m[:, 3:4], in_=m[:, 2:3]), r=[("mv", b)], w=[("mv", b)])
            P.op("dve", lambda e, m=m, y=y: e.tensor_scalar(out=y, in0=y, scalar1=m[:, 0:1], scalar2=m[:, 3:4],
                                                            op0=ALU.subtract, op1=ALU.mult),
                 r=[("y", b), ("mv", b)], w=[("y", b)])
            P.op("pool", lambda e, y=y: e.tensor_tensor(out=y, in0=y, in1=lg, op=ALU.mult), r=[("y", b), "lg"],
                 w=[("y", b)])
            P.op("dve", lambda e, x=x, y=y: e.tensor_tensor(out=x, in0=y, in1=lb, op=ALU.add),
                 r=[("y", b), "lb", ("x", b)], w=[("x", b)])
            if dst_map is None:
                drows = xdst[tb * 128:(tb + 1) * 128, :]
            else:
                r0 = dst_map[tl] * 512 + q * 128
                drows = xdst[r0:r0 + 128, :]
            P.dma("sp", drows, x, r=[("x", b)], w=[], is_out=(dst_map is not None))
            if modulate:
                P.op("pool", lambda e, x=x, y=y: e.tensor_tensor(out=y, in0=x, in1=scp, op=ALU.mult),
                     r=[("x", b), "scp", ("y", b)], w=[("y", b)])
                P.op("dve", lambda e, y=y: e.tensor_tensor(out=y, in0=y, in1=sh, op=ALU.add),
                     r=[("y", b), "sh"], w=[("y", b)])
                for g in range(KC // 4):
                    pb = g % 2
                    for j in range(4):
                        fc = g * 4 + j
                        P.op("pe", lambda e, y=y, fc=fc, pb=pb, j=j: e.transpose(
                            out=PS[pb][:, j * 128:(j + 1) * 128], in_=y[:, fc * 128:(fc + 1) * 128],
                            identity=ident), r=[("y", b), "ident"], w=[("pst", pb)])
                    dst = ub[b][:, g * 4:(g + 1) * 4, :]
                    src = PS[pb][:, :].rearrange("p (a b) -> p a b", b=128)
                    if g % 2 == 0:
                        P.op("act", lambda e, dst=dst, src=src: e.copy(out=dst, in_=src), r=[("pst", pb)],
                             w=[("ub", b)])
                    else:
                        P.op("dve", lambda e, dst=dst, src=src: e.tensor_copy(out=dst, in_=src), r=[("pst", pb)],
                             w=[("ub", b)])
                P.dma("sp", S["uT"][:, tb * 128:(tb + 1) * 128].rearrange("(kc p) t -> p kc t", p=128), ub[b],
                      r=[("ub", b)], w=[])


PEER_MARGIN = 2e-4


def stage_wq(k, l, own):
    P, sb, I, S = k.P, k.sb, k.I, k.S
    for grp in own_groups(own):
        sb.reset()
        sg = dict(name="qT", c0=0, c1=2048, mode="F", out=S["qT"], tiles=list(grp), odt=F32)
        linear(k, S["uT"], grp, D, lambda c0, c1: I["peer_wq"][l, :, c0:c1], [sg])
        P.barrier()


def stage_peer_sel(k, l, own):
    P, sb, PS, I, S = k.P, k.sb, k.PS, k.I, k.S
    sb.reset()
    ident = sb.f32(128)
    skT = sb.f32(16 * 128).rearrange("p (a b) -> p a b", b=128)
    P.dma("sp", ident, I["ident"], w=["ident"])
    P.dma("sp", skT, I["peer_skT"][l].rearrange("a d k -> d a k"), w=["skT"])
    qt = [sb.f32(16 * 128).rearrange("p (a b) -> p a b", b=128) for _ in range(2)]
    s12 = [sb.f32(256) for _ in range(2)]
    tmp = [sb.f32(256) for _ in range(2)]
    m12 = [sb.f32(32) for _ in range(2)]
    cand = [sb.f32(256) for _ in range(2)]
    c16 = [sb.f32(16) for _ in range(2)]
    sm = [sb.f32(8) for _ in range(2)]
    o3 = [sb.f32(384) for _ in range(2)]
    o3t = [sb.f32(384) for _ in range(2)]
    hlb = [sb.bf16(512).rearrange("p (a b) -> p a b", b=128) for _ in range(2)]
    c = 0
    for tl in own:
        for q in range(4):
            tb = tl * 4 + q
            qb = tb % 2
            P.dma("sp", qt[qb], S["qT"][:, tb * 128:(tb + 1) * 128].rearrange("(c p) t -> p c t", p=128),
                  w=[("qt", qb)])
            for h in range(8):
                b = c % 2
                c += 1
                ps = PS[b]
                for p2 in range(2):
                    P.op("pe", lambda e, ps=ps, p2=p2, h=h, qb=qb: e.matmul(
                        ps[:, p2 * 128:(p2 + 1) * 128], qt[qb][:, 2 * h + p2, :], skT[:, 2 * h + p2, :],
                        start=(p2 == 0), stop=True, skip_group_check=True),
                        r=[("qt", qb), "skT"], w=[("pss", b)])
                s = s12[b]
                P.op("act", lambda e, s=s, ps=ps: e.copy(out=s, in_=ps[:, 0:256]), r=[("pss", b)], w=[("s12", b)])
                t = tmp[b]
                m = m12[b]
                for p2 in range(2):
                    sv = s[:, p2 * 128:(p2 + 1) * 128]
                    tv = t[:, p2 * 128:(p2 + 1) * 128]
                    P.op("dve", lambda e, m=m, sv=sv, p2=p2: e.max(out=m[:, p2 * 16:p2 * 16 + 8], in_=sv),
                         r=[("s12", b)], w=[("m12", b)])
                    P.op("dve", lambda e, m=m, sv=sv, tv=tv, p2=p2: e.match_replace(
                        out=tv, in_to_replace=m[:, p2 * 16:p2 * 16 + 8], in_values=sv, imm_value=-3.0e38),
                        r=[("s12", b), ("m12", b)], w=[("tmp", b)])
                    P.op("dve", lambda e, m=m, tv=tv, p2=p2: e.max(out=m[:, p2 * 16 + 8:p2 * 16 + 16], in_=tv),
                         r=[("tmp", b)], w=[("m12", b)])
                cd = cand[b]
                for a in range(16):
                    P.op("dve", lambda e, cd=cd, m=m, a=a: e.tensor_scalar(
                        out=cd[:, a * 16:(a + 1) * 16], in0=m[:, 16:32], scalar1=m[:, a:a + 1], scalar2=None,
                        op0=ALU.add), r=[("m12", b)], w=[("cand", b)])
                cc = c16[b]
                P.op("dve", lambda e, cc=cc, cd=cd: e.max(out=cc[:, 0:8], in_=cd), r=[("cand", b)], w=[("c16", b)])
                P.op("dve", lambda e, cc=cc, cd=cd, t=t: e.match_replace(out=t, in_to_replace=cc[:, 0:8],
                                                                         in_values=cd, imm_value=-3.0e38),
                     r=[("cand", b), ("c16", b)], w=[("tmp", b)])
                P.op("dve", lambda e, cc=cc, t=t: e.max(out=cc[:, 8:16], in_=t), r=[("tmp", b)], w=[("c16", b)])
                z = sm[b]
                P.op("dve", lambda e, z=z, cc=cc: e.tensor_scalar(out=z[:, 0:1], in0=cc[:, 0:1], scalar1=-1.0,
                                                                  scalar2=None, op0=ALU.mult),
                     r=[("c16", b)], w=[("sm", b)])
                P.op("act", lambda e, z=z, cc=cc, t=t: e.activation(out=t[:, 0:16], in_=cc, func=AF.Exp,
                                                                    bias=z[:, 0:1], accum_out=z[:, 1:2]),
                     r=[("c16", b), ("sm", b), ("tmp", b)], w=[("sm", b), ("tmp", b)])
                P.op("act", lambda e, z=z: e.activation(out=z[:, 2:3], in_=z[:, 1:2], func=AF.Ln),
                     r=[("sm", b)], w=[("sm", b)])
                P.op("dve", lambda e, z=z, cc=cc: e.tensor_tensor(out=z[:, 2:3], in0=z[:, 2:3], in1=cc[:, 0:1],
                                                                  op=ALU.add), r=[("sm", b), ("c16", b)],
                     w=[("sm", b)])
                P.op("dve", lambda e, z=z, cc=cc: e.tensor_scalar(out=z[:, 3:4], in0=cc[:, 15:16],
                                                                  scalar1=-PEER_MARGIN, scalar2=None, op0=ALU.add),
                     r=[("c16", b)], w=[("sm", b)])
                o = o3[b]
                P.op("dve", lambda e, o=o, s=s, z=z: e.tensor_scalar(out=o[:, 0:128], in0=s[:, 0:128],
                                                                     scalar1=z[:, 3:4], scalar2=None,
                                                                     op0=ALU.subtract),
                     r=[("s12", b), ("sm", b)], w=[("o3", b)])
                P.op("dve", lambda e, o=o, s=s, z=z: e.tensor_scalar(out=o[:, 128:256], in0=s[:, 0:128],
                                                                     scalar1=z[:, 2:3], scalar2=None,
                                                                     op0=ALU.subtract),
                     r=[("s12", b), ("sm", b)], w=[("o3", b)])
                P.op("pool", lambda e, o=o, s=s: e.tensor_copy(out=o[:, 256:384], in_=s[:, 128:256]),
                     r=[("s12", b)], w=[("o3", b)])
                pt = PS[2 + b]
                for j in range(3):
                    P.op("pe", lambda e, pt=pt, o=o, j=j: e.transpose(out=pt[:, j * 128:(j + 1) * 128],
                                                                     in_=o[:, j * 128:(j + 1) * 128],
                                                                     identity=ident),
                         r=[("o3", b), "ident"], w=[("pst", b)])
                ot = o3t[b]
                P.op("act", lambda e, ot=ot, pt=pt: e.copy(out=ot, in_=pt[:, 0:384]), r=[("pst", b)],
                     w=[("o3t", b)])
                hl = hlb[b]
                for j, eng in ((0, "dve"), (1, "pool")):
                    src = ot[:, j * 128:(j + 1) * 128]
                    P.op(eng, lambda e, hl=hl, src=src, j=j: e.tensor_copy(out=hl[:, 2 * j, :], in_=src),
                         r=[("o3t", b)], w=[("hl", b, j)])
                    P.op(eng, lambda e, hl=hl, src=src, j=j: e.tensor_tensor(out=hl[:, 2 * j + 1, :], in0=src,
                                                                             in1=hl[:, 2 * j, :], op=ALU.subtract),
                         r=[("o3t", b), ("hl", b, j)], w=[("hl", b, j)])
                    for q2 in range(2):
                        P.dma("act", S["S1H"][2 * j + q2, h, :, tb * 128:(tb + 1) * 128], hl[:, 2 * j + q2, :],
                              r=[("hl", b, j)], w=[])
                P.dma("sp", S["S2T"][h, :, tb * 128:(tb + 1) * 128], ot[:, 256:384], r=[("o3t", b)], w=[])


def stage_peer_dense(k, l, own):
    P, sb, PS, I, S = k.P, k.sb, k.PS, k.I, k.S
    for grp in own_groups(own):
        ng = len(grp)
        NTK = 512 * ng
        sb.reset()
        ident = sb.f32(128)
        OH = sb.bf16(8 * 128, parts=16).rearrange("p (a b) -> p a b", b=128)
        P.dma("sp", ident, I["ident"], w=["ident"])
        P.dma("sp", OH, I["oh16"], w=["OH"])
        XT = sb.bf16(KC * NTK).rearrange("p (a b) -> p a b", b=NTK)
        s2T = sb.f32(8 * NTK).rearrange("p (a b) -> p a b", b=NTK)
        for gi, t in enumerate(grp):
            ts = slice(t * 512, (t + 1) * 512)
            P.dma("sp", XT[:, :, gi * 512:(gi + 1) * 512], S["uT"][:, ts].rearrange("(kc p) t -> p kc t", p=128),
                  w=["XT"])
            P.dma("act", s2T[:, :, gi * 512:(gi + 1) * 512], S["S2T"][:, :, ts].rearrange("h j t -> j h t"),
                  w=["s2T"])
        Ub = [sb.f32(D) for _ in range(2)]
        RA = [sb.bf16(NTK, parts=16) for _ in range(2)]
        RL = [sb.bf16(NTK, parts=16) for _ in range(2)]
        UT = sb.bf16(KC * 128).rearrange("p (a b) -> p a b", b=128)
        ga32 = sb.f32(NTK)
        zA = [sb.f32(NTK) for _ in range(2)]
        zB = [sb.f32(NTK) for _ in range(2)]
        Wacc = sb.f32(NTK)
        GAb = [sb.bf16(NTK) for _ in range(2)]
        nh = NTK // 512
        zc = 0
        ga32b = [ga32, sb.f32(NTK)]

        def load_rows(i):
            b = i % 2
            P.dma("sp", Ub[b], I["peer_u"][l, i * 128:(i + 1) * 128, :], w=[("Ub", b)])
            for gi, t in enumerate(grp):
                ts = slice(t * 512, (t + 1) * 512)
                for q2 in range(2):
                    P.dma("sp", RA[b][q2 * 8:(q2 + 1) * 8, gi * 512:(gi + 1) * 512], S["S1H"][q2, :, i, ts],
                          w=[("RA", b)])
                    P.dma("sp", RL[b][q2 * 8:(q2 + 1) * 8, gi * 512:(gi + 1) * 512], S["S1H"][2 + q2, :, i, ts],
                          w=[("RL", b)])

        def front_chunk(i, g):
            b = i % 2
            pb = 6 + g % 2
            for j in range(4):
                kc = g * 4 + j
                P.op("pe", lambda e, pb=pb, j=j, kc=kc, b=b: e.transpose(
                    out=PS[pb][:, j * 128:(j + 1) * 128], in_=Ub[b][:, kc * 128:(kc + 1) * 128],
                    identity=ident), r=[("Ub", b), "ident"], w=[("pst", pb)])
            dst = UT[:, g * 4:(g + 1) * 4, :]
            src = PS[pb][:, :].rearrange("p (a b) -> p a b", b=128)
            if g % 2 == 0:
                P.op("act", lambda e, dst=dst, src=src: e.copy(out=dst, in_=src), r=[("pst", pb)], w=[("UT", g)])
            else:
                P.op("dve", lambda e, dst=dst, src=src: e.tensor_copy(out=dst, in_=src), r=[("pst", pb)],
                     w=[("UT", g)])
            for hf in range(nh):
                for j in range(4):
                    kc = g * 4 + j
                    P.op("pe", lambda e, hf=hf, kc=kc: e.matmul(PS[hf], UT[:, kc, :],
                                                                XT[:, kc, hf * 512:(hf + 1) * 512],
                                                                start=(kc == 0), stop=(kc == KC - 1)),
                         r=[("UT", g), "XT"], w=[("psA", hf)])

        def front_gelu(i):
            gb = ga32b[i % 2]
            for hf in range(nh):
                P.op("act", lambda e, hf=hf, gb=gb: e.activation(out=gb[:, hf * 512:(hf + 1) * 512], in_=PS[hf],
                                                                 func=GELU), r=[("psA", hf)], w=[("ga32", i % 2)])

        load_rows(0)
        for g in range(8):
            front_chunk(0, g)
        front_gelu(0)
        for i in range(128):
            b = i % 2
            if i + 1 < 128:
                load_rows(i + 1)
            for h in range(8):
                zb = zc % 2
                zc += 1
                for hf in range(nh):
                    P.op("pe", lambda e, h=h, hf=hf, b=b: e.matmul(PS[2 + hf], OH[:, h, :],
                                                                   RA[b][:, hf * 512:(hf + 1) * 512],
                                                                   start=True, stop=True),
                         r=["OH", ("RA", b)], w=[("psbA", hf)])
                    P.op("pe", lambda e, h=h, hf=hf, b=b: e.matmul(PS[4 + hf], OH[:, h, :],
                                                                   RL[b][:, hf * 512:(hf + 1) * 512],
                                                                   start=True, stop=True),
                         r=["OH", ("RL", b)], w=[("psbL", hf)])
                    P.op("dve", lambda e, h=h, hf=hf, zb=zb: e.tensor_tensor(
                        out=zA[zb][:, hf * 512:(hf + 1) * 512], in0=PS[2 + hf],
                        in1=s2T[:, h, hf * 512:(hf + 1) * 512], op=ALU.add),
                        r=[("psbA", hf), "s2T"], w=[("zA", zb)])
                    P.op("act", lambda e, hf=hf, zb=zb: e.copy(out=zB[zb][:, hf * 512:(hf + 1) * 512],
                                                               in_=PS[4 + hf]),
                         r=[("psbL", hf)], w=[("zB", zb)])
                if i + 1 < 128:
                    front_chunk(i + 1, h)
                P.op("pool", lambda e, h=h, zb=zb: e.tensor_tensor(out=zB[zb], in0=zB[zb], in1=s2T[:, h, :],
                                                                   op=ALU.add), r=[("zB", zb), "s2T"],
                     w=[("zB", zb)])
                P.op("act", lambda e, zb=zb: e.activation(out=zB[zb], in_=zB[zb], func=AF.Exp), r=[("zB", zb)],
                     w=[("zB", zb)])
                if h == 0:
                    P.op("dve", lambda e, zb=zb: e.scalar_tensor_tensor(out=Wacc, in0=zA[zb], scalar=0.0,
                                                                        in1=zB[zb], op0=ALU.is_ge, op1=ALU.mult),
                         r=[("zA", zb), ("zB", zb)], w=["Wacc"])
                else:
                    P.op("dve", lambda e, zb=zb: e.scalar_tensor_tensor(out=zA[zb], in0=zA[zb], scalar=0.0,
                                                                        in1=zB[zb], op0=ALU.is_ge, op1=ALU.mult),
                         r=[("zA", zb), ("zB", zb)], w=[("zA", zb)])
                    P.op("pool" if h % 2 == 1 else "dve",
                         lambda e, zb=zb: e.tensor_tensor(out=Wacc, in0=Wacc, in1=zA[zb], op=ALU.add),
                         r=[("zA", zb), "Wacc"], w=["Wacc"])
            ga32 = ga32b[i % 2]
            P.op("dve", lambda e, b=b, ga32=ga32: e.tensor_tensor(out=GAb[b], in0=ga32, in1=Wacc, op=ALU.mult),
                 r=[("ga32", i % 2), "Wacc"], w=[("GAb", b)])
            if i + 1 < 128:
                front_gelu(i + 1)
            for gi, t in enumerate(grp):
                P.dma("sp", S["GA"][i, :, t * 512:(t + 1) * 512], GAb[b][:, gi * 512:(gi + 1) * 512],
                      r=[("GAb", b)], w=[])
        P.barrier()
        sb.reset()
        ntb = NTK // 128
        EC = 8
        HC = 2048
        accsb = sb.f32(ntb * HC).rearrange("p (a b) -> p a b", b=HC)
        Vc = [sb.bf16(EC * HC).rearrange("p (a b) -> p a b", b=HC) for _ in range(2)]
        GAc = [sb.bf16(EC * NTK).rearrange("p (a b) -> p a b", b=NTK) for _ in range(2)]
        cc = 0
        pc = 0
        for half in range(D // HC):
            for ch in range(128 // EC):
                b = cc % 2
                cc += 1
                for i in range(EC):
                    e0 = (ch * EC + i) * 128
                    P.dma("pool", Vc[b][:, i, :], I["peer_v"][l, e0:e0 + 128, half * HC:(half + 1) * HC],
                          w=[("Vc", b)])
                for gi, t in enumerate(grp):
                    P.dma("sp", GAc[b][:, :, gi * 512:(gi + 1) * 512],
                          S["GA"][ch * EC:(ch + 1) * EC, :, t * 512:(t + 1) * 512].rearrange("i e t -> e i t"),
                          w=[("GAc", b)])
                for tb in range(ntb):
                    for cq in range(HC // 512):
                        pi = pc % 8
                        pc += 1
                        for i in range(EC):
                            P.op("pe", lambda e, pi=pi, b=b, i=i, tb=tb, cq=cq: e.matmul(
                                PS[pi], GAc[b][:, i, tb * 128:(tb + 1) * 128], Vc[b][:, i, cq * 512:(cq + 1) * 512],
                                start=(i == 0), stop=(i == EC - 1)),
                                r=[("GAc", b), ("Vc", b)], w=[("psY", pi)])
                        dst = accsb[:, tb, cq * 512:(cq + 1) * 512]
                        if ch == 0:
                            P.op("act", lambda e, dst=dst, pi=pi: e.copy(out=dst, in_=PS[pi]), r=[("psY", pi)],
                                 w=[("accsb", tb, cq)])
                        else:
                            P.op("dve", lambda e, dst=dst, pi=pi: e.tensor_tensor(out=dst, in0=dst, in1=PS[pi],
                                                                                  op=ALU.add),
                                 r=[("psY", pi), ("accsb", tb, cq)], w=[("accsb", tb, cq)])
            for tb in range(ntb):
                t = grp[tb // 4]
                r0 = t * 512 + (tb % 4) * 128
                P.dma("sp", S["y2"][r0:r0 + 128, half * HC:(half + 1) * HC], accsb[:, tb, :],
                      r=[("accsb", tb, cq) for cq in range(HC // 512)], w=[("y2out", tb)])
        P.barrier()


_NC_CACHE = {}


def _get_nc():
    if "k" not in _NC_CACHE:
        _NC_CACHE["k"] = build(layers=(0, 1), final_out=True)
    return _NC_CACHE["k"]


def kernel(x, c, w_ada, b_ada, w_in, b_forget, cmp_pe, cmp_w1, cmp_w2, w_branch, w_gate, b_gate, w_out,
           ln1_g, ln1_b, peer_wq, peer_subkeys, peer_u, peer_v, ln2_g, ln2_b):
    from concourse.bass_utils import run_bass_kernel_spmd
    f = lambda a: np.ascontiguousarray(np.asarray(a, dtype=np.float32))
    x = f(x)
    c = f(c)
    B = x.shape[0]
    shared = dict(w_ada=f(w_ada), b_ada=f(b_ada), w_in=f(w_in), b_forget=f(b_forget), cmp_w1=f(cmp_w1),
                  cmp_w2=f(cmp_w2), w_branch=f(w_branch), w_gate=f(w_gate), w_out=f(w_out), ln1_g=f(ln1_g),
                  ln1_b=f(ln1_b), ln2_g=f(ln2_g), ln2_b=f(ln2_b), peer_wq=f(peer_wq), peer_u=f(peer_u),
                  peer_v=f(peer_v))
    shared["cmp_peT"] = np.ascontiguousarray(f(cmp_pe).transpose(0, 1, 3, 2))
    shared["b_gateT"] = np.ascontiguousarray(f(b_gate).reshape(L, 96, 128).transpose(0, 2, 1))
    shared["peer_skT"] = np.ascontiguousarray(f(peer_subkeys).reshape(L, 16, 128, 128).transpose(0, 1, 3, 2))
    import ml_dtypes
    oh = np.zeros((16, 8, 128), np.float32)
    for i in range(16):
        oh[i, i % 8, :] = 1.0
    shared["oh16"] = oh.astype(ml_dtypes.bfloat16)
    tables = [host_tables(0), host_tables(1)]
    k = _get_nc()
    names = [a.memorylocations[0].name for a in k.nc.m.functions[0].allocations
             if isinstance(a, mybir.MemoryLocationSet) and a.kind == "ExternalInput"]
    maps = []
    for b in range(B):
        for h in (0, 1):
            xc = np.zeros((T, D), np.float32)
            if h == 0:
                xc[512:] = x[b, :1536]
            else:
                xc[:] = x[b]
            m = dict(shared)
            m.update(tables[h])
            m["xctx"] = xc
            m["cT"] = np.ascontiguousarray(c[b].reshape(KC, 128).T)
            maps.append({nm: m[nm] for nm in names if nm in m})
    res = run_bass_kernel_spmd(k.nc, maps, core_ids=list(range(2 * B)))
    out = np.zeros((B, 2048, D), np.float32)
    for b in range(B):
        for h in (0, 1):
            o = np.asarray(res.results[b * 2 + h]["out"], dtype=np.float32)
            out[b, 512 * h:512 * h + 512] = o[0:512]
            out[b, 1024 + 512 * h:1024 + 512 * h + 512] = o[512:1024]
    return out
```
